# Optimizing a Trainium2 kernel written in Bass

```python
import math
import jax, jax.numpy as jnp
from jax import lax
import numpy as np

D_MODEL = 4096
BATCH = 4
SEQ = 2048
DEPTH = 2

HEAD_DIM = 128
CONV_W = 4
GDN_HEADS = D_MODEL // 256
GDN_WIDTH = GDN_HEADS * HEAD_DIM
GDN_CHUNK = 64
RG_WIDTH = D_MODEL // 2
RG_BLOCK = 128
RG_BLOCKS = RG_WIDTH // RG_BLOCK
LRU_C = 8.0
SB_HEADS = D_MODEL // 256
SB_WIDTH = SB_HEADS * HEAD_DIM
SB_BLOCK = 128
S5_WIDTH = D_MODEL // 4
S5_GROUP = 16
S5_GROUPS = S5_WIDTH // S5_GROUP
S5_STATE = 64
PEER_HEADS = 8
PEER_NKEYS = 128
PEER_EXPERTS = PEER_NKEYS * PEER_NKEYS
PEER_QDIM = 256
PEER_HALF = PEER_QDIM // 2
PEER_TOPK = 16
PEER_TOKEN_BLOCK = 64
N_EVEN = (DEPTH + 1) // 2
N_ODD = DEPTH // 2
DN_ALPHA = (2.0 * DEPTH) ** 0.25
DN_BETA = (8.0 * DEPTH) ** -0.25
LN_EPS = 1e-5
RMS_EPS = 1e-6
EVEN_IN = 4 * GDN_WIDTH + 2 * GDN_HEADS + 2 * RG_WIDTH
EVEN_MIX = GDN_WIDTH + RG_WIDTH
ODD_IN = 3 * SB_WIDTH + S5_WIDTH
ODD_MIX = SB_WIDTH + S5_WIDTH

kernel_name = 'hybrid_gdn_rglru_stickbreak_s5_peer'


def _split(t, sizes):
    idx = [int(i) for i in np.cumsum(sizes)[:-1]]
    return jnp.split(t, idx, axis=-1)


def layer_norm(x, g, b):
    xf = x.astype(jnp.float32)
    mu = jnp.mean(xf, -1, keepdims=True)
    var = jnp.mean(jnp.square(xf - mu), -1, keepdims=True)
    return (xf - mu) * lax.rsqrt(var + LN_EPS) * g.astype(jnp.float32) + b.astype(jnp.float32)


def l2norm(t):
    return t * lax.rsqrt(jnp.sum(t * t, -1, keepdims=True) + RMS_EPS)


def causal_dwconv(x, w):
    S_ = x.shape[1]
    xp = jnp.pad(x, ((0, 0), (CONV_W - 1, 0), (0, 0)))
    return sum(w[k] * xp[:, k:k + S_] for k in range(CONV_W))


def _linear_combine(e1, e2):
    a1, b1 = e1
    a2, b2 = e2
    return a1 * a2, a2 * b1 + b2


def _complex_combine(e1, e2):
    a1r, a1i, b1r, b1i = e1
    a2r, a2i, b2r, b2i = e2
    return (a2r * a1r - a2i * a1i, a2r * a1i + a2i * a1r,
            a2r * b1r - a2i * b1i + b2r, a2r * b1i + a2i * b1r + b2i)


def gated_delta_rule(q, k, v, g, beta):
    B_, S_, H, dk = q.shape
    dv = v.shape[-1]
    C = GDN_CHUNK
    n = S_ // C

    def to_chunks(t):
        t = t.reshape((B_, n, C, H) + t.shape[3:])
        return t.transpose((1, 0, 3, 2) + tuple(range(4, t.ndim)))

    q, k, v, g, beta = [to_chunks(t) for t in (q, k, v, g, beta)]
    gc = jnp.cumsum(g, axis=-1)
    idx = jnp.arange(C)
    causal = idx[:, None] >= idx[None, :]
    gamma = jnp.exp(jnp.where(causal, gc[..., :, None] - gc[..., None, :], -jnp.inf))
    kb = k * beta[..., None]
    m = jnp.einsum('nbhid,nbhjd->nbhij', kb, k) * gamma
    m = jnp.where(idx[:, None] > idx[None, :], m, 0.0)
    eye = jnp.eye(C, dtype=m.dtype)
    t_inv = lax.linalg.triangular_solve(m + eye, jnp.broadcast_to(eye, m.shape),
                                        left_side=True, lower=True, unit_diagonal=True)
    u = t_inv @ (v * beta[..., None])
    w = t_inv @ (kb * jnp.exp(gc)[..., None])
    a_qk = jnp.einsum('nbhid,nbhjd->nbhij', q, k) * gamma
    g_last = gc[..., -1]
    k_tail = k * jnp.exp(g_last[..., None] - gc)[..., None]
    q_dec = q * jnp.exp(gc)[..., None]

    def step(state, inp):
        u_i, w_i, qd_i, a_i, kt_i, gl_i = inp
        v_new = u_i - w_i @ state
        o_i = qd_i @ state + a_i @ v_new
        state = state * jnp.exp(gl_i)[..., None, None] + jnp.swapaxes(kt_i, -1, -2) @ v_new
        return state, o_i

    s0 = jnp.zeros((B_, H, dk, dv), jnp.float32)
    _, o = lax.scan(step, s0, (u, w, q_dec, a_qk, k_tail, g_last))
    return o.transpose(1, 0, 3, 2, 4).reshape(B_, S_, H, dv)


def even_mixer(x, w_in, gdn_conv_w, gdn_A_log, gdn_dt_bias, gdn_norm_w,
               rg_conv_w, rg_conv_b, rg_wa, rg_ba, rg_wx, rg_bx, rg_lambda, w_out):
    f32 = jnp.float32
    B_, S_, _ = x.shape
    proj = x @ w_in
    qkv, z, a_raw, b_raw, rg_in, rg_gate = _split(
        proj, [3 * GDN_WIDTH, GDN_WIDTH, GDN_HEADS, GDN_HEADS, RG_WIDTH, RG_WIDTH])
    qkv = jax.nn.silu(causal_dwconv(qkv, gdn_conv_w).astype(f32))
    q, k, v = [t.reshape(B_, S_, GDN_HEADS, HEAD_DIM) for t in jnp.split(qkv, 3, axis=-1)]
    q = l2norm(q) * HEAD_DIM ** -0.5
    k = l2norm(k)
    beta = jax.nn.sigmoid(b_raw.astype(f32))
    g = -jnp.exp(gdn_A_log.astype(f32)) * jax.nn.softplus(a_raw.astype(f32) + gdn_dt_bias.astype(f32))
    o = gated_delta_rule(q, k, v, g, beta)
    o = o * lax.rsqrt(jnp.mean(o * o, -1, keepdims=True) + RMS_EPS) * gdn_norm_w.astype(f32)
    o = o * jax.nn.silu(z.astype(f32).reshape(B_, S_, GDN_HEADS, HEAD_DIM))
    gdn_out = o.reshape(B_, S_, GDN_WIDTH)
    xr = (causal_dwconv(rg_in, rg_conv_w) + rg_conv_b).astype(f32)
    xb = xr.reshape(B_, S_, RG_BLOCKS, RG_BLOCK)
    r = jax.nn.sigmoid(jnp.einsum('bsni,nij->bsnj', xb, rg_wa.astype(f32)).reshape(B_, S_, RG_WIDTH)
                       + rg_ba.astype(f32))
    i_g = jax.nn.sigmoid(jnp.einsum('bsni,nij->bsnj', xb, rg_wx.astype(f32)).reshape(B_, S_, RG_WIDTH)
                         + rg_bx.astype(f32))
    log_a = -LRU_C * r * jax.nn.softplus(-rg_lambda.astype(f32))
    a = jnp.exp(log_a)
    b = jnp.sqrt(jnp.maximum(-jnp.expm1(2.0 * log_a), 0.0)) * (i_g * xr)
    _, h = lax.associative_scan(_linear_combine, (a, b), axis=1)
    rg_out = h * jax.nn.gelu(rg_gate.astype(f32))
    mixed = jnp.concatenate([gdn_out, rg_out], axis=-1).astype(w_out.dtype)
    return mixed @ w_out


def stick_breaking_attention(q, k, v):
    S_ = q.shape[2]
    d = q.shape[-1]
    outs = []
    for blk in range(S_ // SB_BLOCK):
        t0 = blk * SB_BLOCK
        end = t0 + SB_BLOCK
        z = jnp.einsum('bhtd,bhsd->bhts', q[:, :, t0:end], k[:, :, :end]) * d ** -0.5
        t_pos = t0 + jnp.arange(SB_BLOCK)
        s_pos = jnp.arange(end)
        mask = s_pos[None, :] < t_pos[:, None]
        log1m = jnp.where(mask, jax.nn.log_sigmoid(-z), 0.0)
        tail = lax.cumsum(log1m, axis=3, reverse=True) - log1m
        wts = jnp.where(mask, jnp.exp(jax.nn.log_sigmoid(z) + tail), 0.0)
        outs.append(jnp.einsum('bhts,bhsd->bhtd', wts, v[:, :, :end]))
    return jnp.concatenate(outs, axis=2)


def s5_ssm(u, A_re, A_im, log_dt, B_re, B_im, C_re, C_im, D):
    f32 = jnp.float32
    B_, S_, _ = u.shape
    ug = u.reshape(B_, S_, S5_GROUPS, S5_GROUP)
    dt = jnp.exp(log_dt.astype(f32))[:, None]
    lr = jnp.minimum(A_re.astype(f32), -1e-4)
    li = A_im.astype(f32)
    mag = jnp.exp(lr * dt)
    ab_re = mag * jnp.cos(li * dt)
    ab_im = mag * jnp.sin(li * dt)
    den = lr * lr + li * li
    nr = ab_re - 1.0
    ni = ab_im
    cr = (nr * lr + ni * li) / den
    ci = (ni * lr - nr * li) / den
    br_, bi_ = B_re.astype(f32), B_im.astype(f32)
    bb_re = cr[..., None] * br_ - ci[..., None] * bi_
    bb_im = cr[..., None] * bi_ + ci[..., None] * br_
    bu_re = jnp.einsum('bsgc,gnc->bsgn', ug, bb_re)
    bu_im = jnp.einsum('bsgc,gnc->bsgn', ug, bb_im)
    a_re = jnp.broadcast_to(ab_re, bu_re.shape)
    a_im = jnp.broadcast_to(ab_im, bu_re.shape)
    _, _, h_re, h_im = lax.associative_scan(_complex_combine, (a_re, a_im, bu_re, bu_im), axis=1)
    y = (jnp.einsum('bsgn,gcn->bsgc', h_re, C_re.astype(f32))
         - jnp.einsum('bsgn,gcn->bsgc', h_im, C_im.astype(f32)))
    return y.reshape(B_, S_, S5_WIDTH) + D.astype(f32) * u


def odd_mixer(x, w_in, s5_A_re, s5_A_im, s5_log_dt, s5_B_re, s5_B_im, s5_C_re, s5_C_im,
              s5_D, s5_glu_w, s5_glu_b, w_out):
    f32 = jnp.float32
    B_, S_, _ = x.shape
    proj = x @ w_in
    q, k, v, u = _split(proj, [SB_WIDTH, SB_WIDTH, SB_WIDTH, S5_WIDTH])
    heads = lambda t: t.astype(f32).reshape(B_, S_, SB_HEADS, HEAD_DIM).transpose(0, 2, 1, 3)
    sb = stick_breaking_attention(heads(q), heads(k), heads(v))
    sb_out = sb.transpose(0, 2, 1, 3).reshape(B_, S_, SB_WIDTH)
    y = s5_ssm(u.astype(f32), s5_A_re, s5_A_im, s5_log_dt, s5_B_re, s5_B_im, s5_C_re, s5_C_im, s5_D)
    yg = jax.nn.gelu(y)
    s5_out = yg * jax.nn.sigmoid(yg @ s5_glu_w.astype(f32) + s5_glu_b.astype(f32))
    mixed = jnp.concatenate([sb_out, s5_out], axis=-1).astype(w_out.dtype)
    return mixed @ w_out


def peer_ffn(x, wq, keys, U, V):
    B_, S_, Dm = x.shape
    T = B_ * S_
    xt = x.reshape(T, Dm)
    q = (xt @ wq).astype(jnp.float32).reshape(T, PEER_HEADS, 2, PEER_HALF)
    kf = keys.astype(jnp.float32)
    s1 = jnp.einsum('thd,hnd->thn', q[:, :, 0], kf[:, 0])
    s2 = jnp.einsum('thd,hnd->thn', q[:, :, 1], kf[:, 1])
    v1, i1 = lax.top_k(s1, PEER_TOPK)
    v2, i2 = lax.top_k(s2, PEER_TOPK)
    cand = (v1[..., :, None] + v2[..., None, :]).reshape(T, PEER_HEADS, PEER_TOPK * PEER_TOPK)
    sc, pos = lax.top_k(cand, PEER_TOPK)
    expert = (jnp.take_along_axis(i1, pos // PEER_TOPK, axis=-1) * PEER_NKEYS
              + jnp.take_along_axis(i2, pos % PEER_TOPK, axis=-1))
    gate = jax.nn.softmax(sc, axis=-1)
    nb = T // PEER_TOKEN_BLOCK

    def block_fn(args):
        xb, eb, gb = args
        act = jax.nn.gelu(jnp.einsum('phkd,pd->phk', U[eb], xb).astype(jnp.float32))
        wts = (gb * act).astype(V.dtype)
        return jnp.einsum('phk,phkd->pd', wts, V[eb])

    out = lax.map(block_fn, (xt.reshape(nb, PEER_TOKEN_BLOCK, Dm),
                             expert.reshape(nb, PEER_TOKEN_BLOCK, PEER_HEADS, PEER_TOPK),
                             gate.reshape(nb, PEER_TOKEN_BLOCK, PEER_HEADS, PEER_TOPK)))
    return out.reshape(B_, S_, Dm)


def setup_inputs(seed: int = 0) -> dict:
    key = jax.random.key(seed)
    ks = iter(jax.random.split(key, 48))
    f32 = jnp.float32
    nrm = lambda shape, std: jax.random.normal(next(ks), shape, f32) * std
    unif = lambda shape, lo, hi: jax.random.uniform(next(ks), shape, f32, lo, hi)
    dt_g = jnp.exp(unif((N_EVEN, GDN_HEADS), math.log(1e-3), math.log(1e-1)))
    p_lru = unif((N_EVEN, RG_WIDTH), 0.9, 0.999) ** (1.0 / LRU_C)
    return {
        'x': nrm((BATCH, SEQ, D_MODEL), 1.0),
        'w_in_e': nrm((N_EVEN, D_MODEL, EVEN_IN), D_MODEL ** -0.5),
        'gdn_conv_w': nrm((N_EVEN, CONV_W, 3 * GDN_WIDTH), CONV_W ** -0.5),
        'gdn_A_log': jnp.log(unif((N_EVEN, GDN_HEADS), 1.0, 16.0)),
        'gdn_dt_bias': dt_g + jnp.log(-jnp.expm1(-dt_g)),
        'gdn_norm_w': 1.0 + nrm((N_EVEN, HEAD_DIM), 0.01),
        'rg_conv_w': nrm((N_EVEN, CONV_W, RG_WIDTH), CONV_W ** -0.5),
        'rg_conv_b': nrm((N_EVEN, RG_WIDTH), 0.01),
        'rg_wa': nrm((N_EVEN, RG_BLOCKS, RG_BLOCK, RG_BLOCK), RG_BLOCK ** -0.5),
        'rg_ba': nrm((N_EVEN, RG_WIDTH), 0.01),
        'rg_wx': nrm((N_EVEN, RG_BLOCKS, RG_BLOCK, RG_BLOCK), RG_BLOCK ** -0.5),
        'rg_bx': nrm((N_EVEN, RG_WIDTH), 0.01),
        'rg_lambda': jnp.log(p_lru) - jnp.log1p(-p_lru),
        'w_out_e': nrm((N_EVEN, EVEN_MIX, D_MODEL), EVEN_MIX ** -0.5 * DN_BETA),
        'w_in_o': nrm((N_ODD, D_MODEL, ODD_IN), D_MODEL ** -0.5),
        's5_A_re': -0.5 + nrm((N_ODD, S5_GROUPS, S5_STATE), 0.01),
        's5_A_im': math.pi * jnp.arange(S5_STATE, dtype=f32) + nrm((N_ODD, S5_GROUPS, S5_STATE), 0.01),
        's5_log_dt': unif((N_ODD, S5_GROUPS), math.log(1e-3), math.log(1e-1)),
        's5_B_re': nrm((N_ODD, S5_GROUPS, S5_STATE, S5_GROUP), (2 * S5_GROUP) ** -0.5),
        's5_B_im': nrm((N_ODD, S5_GROUPS, S5_STATE, S5_GROUP), (2 * S5_GROUP) ** -0.5),
        's5_C_re': nrm((N_ODD, S5_GROUPS, S5_GROUP, S5_STATE), 0.5),
        's5_C_im': nrm((N_ODD, S5_GROUPS, S5_GROUP, S5_STATE), 0.5),
        's5_D': nrm((N_ODD, S5_WIDTH), 0.5),
        's5_glu_w': nrm((N_ODD, S5_WIDTH, S5_WIDTH), S5_WIDTH ** -0.5),
        's5_glu_b': nrm((N_ODD, S5_WIDTH), 0.01),
        'w_out_o': nrm((N_ODD, ODD_MIX, D_MODEL), ODD_MIX ** -0.5 * DN_BETA),
        'ln_mix_g': 1.0 + nrm((DEPTH, D_MODEL), 0.01),
        'ln_mix_b': nrm((DEPTH, D_MODEL), 0.01),
        'peer_wq': nrm((DEPTH, D_MODEL, PEER_HEADS * PEER_QDIM), D_MODEL ** -0.5),
        'peer_keys': nrm((DEPTH, PEER_HEADS, 2, PEER_NKEYS, PEER_HALF), PEER_HALF ** -0.5),
        'peer_u': nrm((DEPTH, PEER_EXPERTS, D_MODEL), D_MODEL ** -0.5),
        'peer_v': nrm((DEPTH, PEER_EXPERTS, D_MODEL), DN_BETA * PEER_HEADS ** -0.5),
        'ln_ffn_g': 1.0 + nrm((DEPTH, D_MODEL), 0.01),
        'ln_ffn_b': nrm((DEPTH, D_MODEL), 0.01),
    }


def reference(x, w_in_e, gdn_conv_w, gdn_A_log, gdn_dt_bias, gdn_norm_w, rg_conv_w, rg_conv_b,
              rg_wa, rg_ba, rg_wx, rg_bx, rg_lambda, w_out_e, w_in_o, s5_A_re, s5_A_im, s5_log_dt,
              s5_B_re, s5_B_im, s5_C_re, s5_C_im, s5_D, s5_glu_w, s5_glu_b, w_out_o,
              ln_mix_g, ln_mix_b, peer_wq, peer_keys, peer_u, peer_v, ln_ffn_g, ln_ffn_b):
    h = x
    for layer in range(DEPTH):
        j = layer // 2
        if layer % 2 == 0:
            mix = even_mixer(h, w_in_e[j], gdn_conv_w[j], gdn_A_log[j], gdn_dt_bias[j], gdn_norm_w[j],
                             rg_conv_w[j], rg_conv_b[j], rg_wa[j], rg_ba[j], rg_wx[j], rg_bx[j],
                             rg_lambda[j], w_out_e[j])
        else:
            mix = odd_mixer(h, w_in_o[j], s5_A_re[j], s5_A_im[j], s5_log_dt[j], s5_B_re[j], s5_B_im[j],
                            s5_C_re[j], s5_C_im[j], s5_D[j], s5_glu_w[j], s5_glu_b[j], w_out_o[j])
        h = layer_norm(DN_ALPHA * h + mix, ln_mix_g[layer], ln_mix_b[layer]).astype(x.dtype)
        ffn = peer_ffn(h, peer_wq[layer], peer_keys[layer], peer_u[layer], peer_v[layer])
        h = layer_norm(DN_ALPHA * h + ffn, ln_ffn_g[layer], ln_ffn_b[layer]).astype(x.dtype)
    return h
```

```python
import math
import contextlib
import numpy as np
import concourse.bass as bass
import concourse.mybir as mybir
from concourse.bass_utils import run_bass_kernel_spmd

F32 = mybir.dt.float32
BF16 = mybir.dt.bfloat16
U32 = mybir.dt.uint32
ALU = mybir.AluOpType
AF = mybir.ActivationFunctionType
AX = mybir.AxisListType

SEM_LIMIT = 30000


class Buf:
    def __init__(self, t, name):
        self.t = t
        self.name = name
        self.state = {}
        self.psum = False

    def __getitem__(self, idx):
        return self.t[idx]


class Op:
    __slots__ = ("eng", "fn", "deps", "dma", "signal", "semv", "idx")


class Prog:
    ENGS = ["pe", "act", "dve", "pool", "sp"]

    def __init__(self, nc):
        self.nc = nc
        self.ops = []
        self.stack = contextlib.ExitStack()
        self.nbuf = 0

    def sbuf(self, shape, dtype, name=None):
        self.nbuf += 1
        name = name or f"sb{self.nbuf}"
        t = self.stack.enter_context(self.nc.sbuf_tensor(name, list(shape), dtype))
        return Buf(t, name)

    def psum(self, shape, dtype=F32, name=None):
        self.nbuf += 1
        name = name or f"ps{self.nbuf}"
        t = self.stack.enter_context(self.nc.psum_tensor(name, list(shape), dtype))
        b = Buf(t, name)
        b.psum = True
        return b

    def dram(self, name, shape, dtype, kind="Internal"):
        t = self.nc.dram_tensor(name, list(shape), dtype, kind=kind)
        return Buf(t, name)

    def add(self, eng, fn, reads=(), writes=(), dma=False):
        op = Op()
        op.eng = eng
        op.fn = fn
        op.dma = dma
        op.signal = False
        op.semv = None
        op.idx = len(self.ops)
        deps = set()
        for acc in reads:
            buf, key = acc if isinstance(acc, tuple) else (acc, None)
            st = buf.state
            keys = list(st.keys()) if key is None else [k for k in (key, None) if k in st]
            for k in keys:
                w = st[k][0]
                if w is not None:
                    deps.add(w)
                if getattr(buf, "psum", False):
                    for re_, rop in st[k][1].items():
                        if re_ != eng:
                            deps.add(rop)
            ent = st.setdefault(key, [None, {}, []])
            if dma:
                ent[2].append(op)
            else:
                ent[1][eng] = op
        for acc in writes:
            buf, key = acc if isinstance(acc, tuple) else (acc, None)
            st = buf.state
            keys = list(st.keys()) if key is None else [k for k in (key, None) if k in st]
            for k in keys:
                w, rd, drd = st[k]
                if w is not None:
                    deps.add(w)
                deps.update(rd.values())
                deps.update(drd)
            if key is None:
                st.clear()
                st[None] = [op, {}, []]
            else:
                st[key] = [op, {}, []]
        deps.discard(op)
        op.deps = deps
        for d in deps:
            d.signal = True
        self.ops.append(op)
        return op

    def dma(self, out_ap, in_ap, reads=(), writes=(), q="sp", **kw):
        return self.add(q, lambda e: e.dma_start(out=out_ap, in_=in_ap, **kw), reads, writes, dma=True)

    def mm(self, out_ap, lhsT, rhs, start, stop, reads, writes, **kw):
        return self.add("pe", lambda e: e.matmul(out_ap, lhsT, rhs, start=start, stop=stop, **kw), reads, writes)

    def emit(self, final_ops):
        nc = self.nc
        ccount = {e: 0 for e in self.ENGS}
        dcount = {e: 0 for e in self.ENGS}
        for op in final_ops:
            op.signal = True
        for op in self.ops:
            if op.dma:
                dcount[op.eng] += 1
                op.semv = ("d", op.eng, dcount[op.eng])
            elif op.signal:
                ccount[op.eng] += 1
                op.semv = ("c", op.eng, ccount[op.eng])
        sems = {}

        DMA_SLOTS = 16

        def getsem(kind, eng, n):
            if kind == "c":
                ep = (n - 1) // SEM_LIMIT
                k = (kind, eng, ep)
                v = n - ep * SEM_LIMIT
            else:
                slot = (n - 1) % DMA_SLOTS
                k = (kind, eng, slot)
                v = ((n - 1) // DMA_SLOTS + 1) * 16
            if k not in sems:
                sems[k] = self.stack.enter_context(nc.semaphore(f"s_{kind}_{eng}_{k[2]}"))
            return sems[k], k, v

        per_eng = {e: [] for e in self.ENGS}
        for op in self.ops:
            per_eng[op.eng].append(op)
        self.nwaits = 0

        def run(eng_name, e):
            known = {}
            for op in per_eng[eng_name]:
                need = {}
                for d in op.deps:
                    if d.eng == "pe" and eng_name == "pe" and not d.dma:
                        continue
                    kind, de, n = d.semv
                    s, k, v = getsem(kind, de, n)
                    if need.get(k, (None, 0))[1] < v:
                        need[k] = (s, v)
                for k, (s, v) in need.items():
                    if known.get(k, 0) >= v:
                        continue
                    e.wait_ge(s, v)
                    self.nwaits += 1
                    known[k] = v
                if op.dma and op.semv[2] > DMA_SLOTS:
                    s, k, v = getsem("d", op.semv[1], op.semv[2] - DMA_SLOTS)
                    if known.get(k, 0) < v:
                        e.wait_ge(s, v)
                        known[k] = v
                ins = op.fn(e)
                if op.semv is not None:
                    kind, de, n = op.semv
                    s, k, v = getsem(kind, de, n)
                    ins.then_inc(s, 16 if kind == "d" else 1)
            if eng_name == "sp":
                for f in final_ops:
                    kind, de, n = f.semv
                    s, k, v = getsem(kind, de, n)
                    e.wait_ge(s, v)

        for op in self.ops:
            if op.semv is not None:
                getsem(*op.semv)
        with nc.Block() as block:
            block.sync(lambda e: run("sp", e))
            block.tensor(lambda e: run("pe", e))
            block.vector(lambda e: run("dve", e))
            block.scalar(lambda e: run("act", e))
            block.gpsimd(lambda e: run("pool", e))
        self.stack.close()


DN_ALPHA = 4.0 ** 0.25
LN_EPS = 1e-5
T_CORE = 1024
TB = 512
NWB = 4


class WStream:
    def __init__(self, P, bufs, depth=3):
        self.P = P
        self.bufs = bufs
        self.pending = []
        self.ready = []
        self.n = 0
        self.depth = depth

    def push(self, ap, width=4096):
        self.pending.append((ap, width))

    def _issue(self):
        ap, width = self.pending.pop(0)
        b = self.bufs[self.n % len(self.bufs)]
        self.n += 1
        self.P.dma(b[:, 0:width], ap, writes=[b], q="pool")
        self.ready.append(b)

    def get(self):
        while self.pending and len(self.ready) < self.depth:
            self._issue()
        b = self.ready.pop(0)
        while self.pending and len(self.ready) < self.depth:
            self._issue()
        return b


def layernorm_T(P, r, KC, g_sb, b_sb, ones32, psS, psQ, sq, stat, hn_bf=None):
    mean, rstd, tmpv = stat
    for kc in range(KC):
        P.mm(psS[:], ones32[:], r[:, kc, :], kc == 0, kc == KC - 1, reads=[ones32, (r, kc)], writes=[psS])
    for kc in range(KC):
        s = sq[kc % 2]
        P.add("act", lambda e, kc=kc, s=s: e.activation(out=s[:], in_=r[:, kc, :], func=AF.Square), reads=[(r, kc)], writes=[s])
        P.mm(psQ[:], ones32[:], s[:], kc == 0, kc == KC - 1, reads=[ones32, s], writes=[psQ])
    D = KC * 128
    P.add("dve", lambda e: e.tensor_scalar(out=mean[:], in0=psS[:], scalar1=1.0 / D, scalar2=None, op0=ALU.mult), reads=[psS], writes=[mean])
    P.add("dve", lambda e: e.tensor_tensor(out=tmpv[:], in0=mean[:], in1=mean[:], op=ALU.mult), reads=[mean], writes=[tmpv])
    P.add("dve", lambda e: e.scalar_tensor_tensor(out=tmpv[:], in0=psQ[:], scalar=1.0 / D, in1=tmpv[:], op0=ALU.mult, op1=ALU.subtract), reads=[psQ, tmpv], writes=[tmpv])
    P.add("dve", lambda e: e.tensor_scalar(out=tmpv[:], in0=tmpv[:], scalar1=LN_EPS, scalar2=None, op0=ALU.add), reads=[tmpv], writes=[tmpv])
    P.add("act", lambda e: e.activation(out=tmpv[:], in_=tmpv[:], func=AF.Sqrt), reads=[tmpv], writes=[tmpv])
    P.add("dve", lambda e: e.reciprocal(out=rstd[:], in_=tmpv[:]), reads=[tmpv], writes=[rstd])
    for kc in range(KC):
        P.add("dve", lambda e, kc=kc: e.tensor_tensor(out=r[:, kc, :], in0=r[:, kc, :], in1=mean[:], op=ALU.subtract), reads=[(r, kc), mean], writes=[(r, kc)])
        P.add("dve", lambda e, kc=kc: e.tensor_tensor(out=r[:, kc, :], in0=r[:, kc, :], in1=rstd[:], op=ALU.mult), reads=[(r, kc), rstd], writes=[(r, kc)])
        P.add("dve", lambda e, kc=kc: e.tensor_scalar(out=r[:, kc, :], in0=r[:, kc, :], scalar1=g_sb[:, kc:kc + 1], scalar2=b_sb[:, kc:kc + 1], op0=ALU.mult, op1=ALU.add), reads=[(r, kc), g_sb, b_sb], writes=[(r, kc)])
        if hn_bf is not None:
            P.add("act", lambda e, kc=kc: e.activation(out=hn_bf[:, kc, :], in_=r[:, kc, :], func=AF.Copy), reads=[(r, kc)], writes=[(hn_bf, kc)])


def build_tail(odd, T=T_CORE, n_ec=128, debug=False):
    nc = bass.Bass("TRN2", target_bir_lowering=False)
    P = Prog(nc)
    FC = 24 if odd else 32
    F = FC * 128
    mixedT = P.dram("mixedT", [F, T], F32, kind="ExternalInput")
    hprevT = P.dram("hprevT", [4096, T], F32, kind="ExternalInput")
    w_out = P.dram("w_out", [32, 128, FC * 128], F32, kind="ExternalInput")
    lnp = P.dram("lnp", [128, 4, 32], F32, kind="ExternalInput")
    wq = P.dram("wq", [16, 128, 4096], F32, kind="ExternalInput")
    keysT = P.dram("keysT", [128, 16, 128], F32, kind="ExternalInput")
    U = P.dram("U", [128, 128, 4096], F32, kind="ExternalInput")
    V = P.dram("V", [16, 8, 128, 4096], F32, kind="ExternalInput")
    consts = P.dram("consts", [128, 2, 128], F32, kind="ExternalInput")
    if odd:
        gluw = P.dram("gluw", [8, 128, 1024], F32, kind="ExternalInput")
        glub = P.dram("glub", [128, 8], F32, kind="ExternalInput")
    outT = P.dram("outT", [4096, T], F32, kind="ExternalOutput")
    hsc = P.dram("hsc", [4096, T], F32)
    if debug:
        dbg_h1 = P.dram("dbg_h1", [4096, T], F32, kind="ExternalOutput")
        dbg_ffn = P.dram("dbg_ffn", [4096, T], F32, kind="ExternalOutput")

    A32 = P.sbuf([128, 32, TB], BF16, "A32")
    B64 = P.sbuf([128, 32, TB], F32, "B64")
    S32 = P.sbuf([128, 4, 16, 128], F32, "S32")
    wbs = [P.sbuf([128, 4096], BF16, f"wb{i}") for i in range(NWB)]
    lnp_sb = P.sbuf([128, 4, 32], F32, "lnp_sb")
    const32 = P.sbuf([128, 2, 128], F32, "const32")
    ident_bf = P.sbuf([128, 128], BF16, "identbf")
    t2k = [P.sbuf([128, TB], F32, f"t2k{i}") for i in range(5)]
    hp = t2k[0:2]
    sq = t2k[0:2]
    stat = t2k[2:5]
    qc = t2k[0:2]
    actb = t2k[0:2]
    WT = [P.sbuf([128, 8, TB], BF16, f"WT{i}") for i in range(2)]
    keys_sb = View(WT[1], WT[1][:].rearrange("p a t -> p (a t)").bitcast(F32).rearrange("p (c n) -> p c n", c=16))
    XM = P.sbuf([128, 2048], F32, "XM")
    ExV = View(XM, XM[:, 0:1024].rearrange("p (h n) -> p h n", h=8))
    MkV = View(XM, XM[:, 1024:2048].rearrange("p (h n) -> p h n", h=8))
    cand = View(XM, XM[:].rearrange("p (h n) -> p h n", h=8))
    tmpb = [P.sbuf([128, 8, 128], BF16, f"tmpb{i}") for i in range(2)]
    mx = P.sbuf([128, 16], F32, "mx")
    ev = P.sbuf([128, 16, 16], F32, "ev")
    t128 = P.sbuf([128, 128], F32, "t128")
    t256 = [P.sbuf([128, 256], F32, f"t256{i}") for i in range(2)]
    c24 = P.sbuf([128, 8, 24], F32, "c24")
    Zs = P.sbuf([128, 8], F32, "Zs")
    rZ = P.sbuf([128, 8], F32, "rZ")
    th = P.sbuf([128, 8], F32, "th")
    thn = P.sbuf([128, 4, 8], F32, "thn")
    if odd:
        s32flat = S32[:].rearrange("p a c n -> p (a c n)")
        s5o = View(S32, s32flat[:, 0:2048].bitcast(BF16).rearrange("p (c t) -> p c t", c=8))
        glw = [View(S32, s32flat[:, 2048 + 512 * i:2048 + 512 * (i + 1)].bitcast(BF16)) for i in range(2)]
        glub_sb = P.sbuf([128, 8], F32, "glub_sb")
    psA = [P.psum([128, 512], F32, f"psA{i}") for i in range(2)]
    psG = [P.psum([128, 512], F32, f"psG{i}") for i in range(2)]
    psO = [P.psum([128, 512], F32, f"psO{i}") for i in range(2)]
    psM = [P.psum([128, 512], F32, f"psM{i}") for i in range(2)]


    P.dma(lnp_sb[:], lnp.t.ap(), writes=[lnp_sb])
    P.dma(const32[:], consts.t.ap(), writes=[const32])
    P.dma(ident_bf[:], consts.t.ap()[:, 0, :], writes=[ident_bf], q="pool")
    if odd:
        P.dma(glub_sb[:], glub.t.ap(), writes=[glub_sb])
    ones_ap = const32[:, 1, :]

    finals = []
    nTB = T // TB
    cnt = 0
    for tb in range(nTB):
        t0 = tb * TB
        ws = WStream(P, wbs)
        mT = mixedT.t.ap().rearrange("(k p) t -> p k t", p=128)
        for k0 in range(0, FC, 8):
            P.dma(A32[:, k0:k0 + 8, :], mT[:, k0:k0 + 8, t0:t0 + TB], writes=[(A32, k) for k in range(k0, k0 + 8)], q="pool")
        if odd:
            for c in range(8):
                gb = glw[c % 2]
                P.dma(gb[:], gluw.t.ap()[c], writes=[gb], q="pool")
                ps = psM[c % 2]
                gv = gb[:].rearrange("p (k m) -> p k m", k=8)
                for kc in range(8):
                    P.mm(ps[:], gv[:, kc, :], A32[:, 16 + kc, :], kc == 0, kc == 7, reads=[gb, A32], writes=[ps])
                s = sq[c % 2]
                P.add("act", lambda e, c=c, ps=ps, s=s: e.activation(out=s[:], in_=ps[:], func=AF.Sigmoid, bias=glub_sb[:, c:c + 1]), reads=[ps, glub_sb], writes=[s])
                P.add("dve", lambda e, c=c, s=s: e.tensor_tensor(out=s5o[:, c, :], in0=s[:], in1=A32[:, 16 + c, :], op=ALU.mult), reads=[s, A32], writes=[s5o])
        for dc in range(32):
            ws.push(w_out.t.ap()[dc], FC * 128)
        for dc in range(32):
            wb = ws.get()
            wv = wb[:, 0:FC * 128].rearrange("p (k m) -> p k m", k=FC)
            ps = psA[dc % 2]
            h = hp[dc % 2]
            P.dma(h[:], hprevT.t.ap()[dc * 128:(dc + 1) * 128, t0:t0 + TB], writes=[h])
            for kc in range(FC):
                if odd and kc >= 16:
                    rhs, rd = s5o[:, kc - 16, :], s5o
                else:
                    rhs, rd = A32[:, kc, :], A32
                P.mm(ps[:], wv[:, kc, :], rhs, kc == 0, kc == FC - 1, reads=[wb, rd], writes=[ps])
            P.add("dve", lambda e, dc=dc, h=h, ps=ps: e.scalar_tensor_tensor(out=B64[:, dc, :], in0=h[:], scalar=DN_ALPHA, in1=ps[:], op0=ALU.mult, op1=ALU.add), reads=[h, ps], writes=[(B64, dc)])
        layernorm_T(P, B64, 32, _V(lnp_sb, 0), _V(lnp_sb, 1), _V2(const32, ones_ap), psM[0], psM[1], sq, stat, hn_bf=A32)
        hv = hsc.t.ap().rearrange("(k p) t -> p k t", p=128)
        for k0 in range(0, 32, 8):
            P.dma(hv[:, k0:k0 + 8, t0:t0 + TB], B64[:, k0:k0 + 8, :], reads=[(B64, k) for k in range(k0, k0 + 8)], writes=[(hsc, k0)])
        if debug:
            dv = dbg_h1.t.ap().rearrange("(k p) t -> p k t", p=128)
            for k0 in range(0, 32, 8):
                finals.append(P.dma(dv[:, k0:k0 + 8, t0:t0 + TB], B64[:, k0:k0 + 8, :], reads=[(B64, k) for k in range(k0, k0 + 8)]))
        P.dma(keys_sb[:], keysT.t.ap(), writes=[keys_sb])
        for c in range(16):
            ws.push(wq.t.ap()[c])
        for c in range(16):
            wb = ws.get()
            wv = wb[:].rearrange("p (k m) -> p k m", k=32)
            ps = psA[c % 2]
            for kc in range(32):
                P.mm(ps[:], wv[:, kc, :], A32[:, kc, :], kc == 0, kc == 31, reads=[wb, A32], writes=[ps])
            q = qc[c % 2]
            P.add("act", lambda e, q=q, ps=ps: e.activation(out=q[:], in_=ps[:], func=AF.Copy), reads=[ps], writes=[q])
            pss = psM[c % 2]
            for tt in range(4):
                P.mm(pss[:, tt * 128:(tt + 1) * 128], q[:, tt * 128:(tt + 1) * 128], keys_sb[:, c, :], True, True, reads=[q, keys_sb], writes=[pss])
            P.add("dve", lambda e, c=c, pss=pss: e.tensor_copy(out=S32[:, :, c, :], in_=pss[:].rearrange("p (a n) -> p a n", a=4)), reads=[pss], writes=[S32])
        S5v = S32[:].rearrange("p a (h two) n -> p a h two n", two=2)
        for tt in range(4):
            Sv = S32[:, tt]
            P.add("dve", lambda e, Sv=Sv: e.tensor_reduce(out=mx[:], in_=Sv, axis=AX.X, op=ALU.max), reads=[S32], writes=[mx])
            P.add("dve", lambda e, Sv=Sv: e.tensor_tensor(out=Sv, in0=Sv, in1=mx[:].unsqueeze(2).to_broadcast([128, 16, 128]), op=ALU.subtract), reads=[S32, mx], writes=[S32])
            P.add("act", lambda e, Sv=Sv: e.activation(out=Sv, in_=Sv, func=AF.Exp), reads=[S32], writes=[S32])
            for c in range(16):
                P.add("dve", lambda e, c=c, Sv=Sv: e.max(out=ev[:, c, 0:8], in_=Sv[:, c, :]), reads=[S32], writes=[ev])
                P.add("dve", lambda e, c=c, Sv=Sv: e.match_replace(out=t128[:], in_to_replace=ev[:, c, 0:8], in_values=Sv[:, c, :], imm_value=-1.0), reads=[S32, ev], writes=[t128])
                P.add("dve", lambda e, c=c: e.max(out=ev[:, c, 8:16], in_=t128[:]), reads=[t128], writes=[ev])
            ev5 = ev[:].rearrange("p (h two) a -> p h two a", two=2)
            c4 = cand[:].rearrange("p h (a b) -> p h a b", a=16)
            P.add("dve", lambda e, ev5=ev5, c4=c4: e.tensor_tensor(out=c4, in0=ev5[:, :, 0, :].unsqueeze(3).to_broadcast([128, 8, 16, 16]), in1=ev5[:, :, 1, :].unsqueeze(2).to_broadcast([128, 8, 16, 16]), op=ALU.mult), reads=[ev], writes=[cand])
            for h in range(8):
                P.add("dve", lambda e, h=h: e.max(out=c24[:, h, 0:8], in_=cand[:, h, :]), reads=[cand], writes=[c24])
                P.add("dve", lambda e, h=h: e.match_replace(out=t256[0][:], in_to_replace=c24[:, h, 0:8], in_values=cand[:, h, :], imm_value=-1.0), reads=[cand, c24], writes=[t256[0]])
                P.add("dve", lambda e, h=h: e.max(out=c24[:, h, 8:16], in_=t256[0][:]), reads=[t256[0]], writes=[c24])
                P.add("dve", lambda e, h=h: e.match_replace(out=t256[1][:], in_to_replace=c24[:, h, 8:16], in_values=t256[0][:], imm_value=-1.0), reads=[t256[0], c24], writes=[t256[1]])
                P.add("dve", lambda e, h=h: e.max(out=c24[:, h, 16:24], in_=t256[1][:]), reads=[t256[1]], writes=[c24])
            P.add("dve", lambda e: e.tensor_reduce(out=Zs[:], in_=c24[:, :, 0:16], axis=AX.X, op=ALU.add), reads=[c24], writes=[Zs])
            P.add("dve", lambda e: e.reciprocal(out=rZ[:], in_=Zs[:]), reads=[Zs], writes=[rZ])
            P.add("dve", lambda e: e.tensor_tensor(out=th[:], in0=c24[:, :, 15], in1=c24[:, :, 16], op=ALU.add), reads=[c24], writes=[th])
            P.add("dve", lambda e, tt=tt: e.scalar_tensor_tensor(out=thn[:, tt, :], in0=th[:], scalar=0.5, in1=rZ[:], op0=ALU.mult, op1=ALU.mult), reads=[th, rZ], writes=[thn])
            P.add("dve", lambda e, tt=tt: e.tensor_tensor(out=S5v[:, tt, :, 0, :], in0=S5v[:, tt, :, 0, :], in1=rZ[:].unsqueeze(2).to_broadcast([128, 8, 128]), op=ALU.mult), reads=[S32, rZ], writes=[S32])
        n_eg = n_ec // 8
        for eg in range(n_eg):
            for j in range(8):
                ws.push(U.t.ap()[eg * 8 + j])
            for db in range(8):
                ws.push(V.t.ap()[eg, db])
        for eg in range(n_eg):
            wt = WT[eg % 2]
            for j in range(8):
                i = eg * 8 + j
                wb = ws.get()
                wv = wb[:].rearrange("p (k m) -> p k m", k=32)
                pa = psA[i % 2]
                for kc in range(32):
                    P.mm(pa[:], wv[:, kc, :], A32[:, kc, :], kc == 0, kc == 31, reads=[wb, A32], writes=[pa])
                ab = actb[i % 2]
                P.add("act", lambda e, ab=ab, pa=pa: e.activation(out=ab[:], in_=pa[:], func=AF.Gelu_apprx_tanh), reads=[pa], writes=[ab])
                pg = psG[i % 2]
                for tt in range(4):
                    k = cnt % 2
                    cnt += 1
                    ex, mk, tb_ = ExV, MkV, tmpb[k]
                    P.add("dve", lambda e, tt=tt, i=i, ex=ex: e.tensor_tensor(out=ex[:], in0=S5v[:, tt, :, 1, :], in1=S5v[:, tt, :, 0, i:i + 1].to_broadcast([128, 8, 128]), op=ALU.mult), reads=[S32], writes=[ex])
                    P.add("dve", lambda e, tt=tt, ex=ex, mk=mk: e.tensor_tensor(out=mk[:], in0=ex[:], in1=thn[:, tt, :].unsqueeze(2).to_broadcast([128, 8, 128]), op=ALU.is_ge), reads=[ex, thn], writes=[mk])
                    P.add("dve", lambda e, ex=ex, mk=mk, tb_=tb_: e.tensor_tensor(out=tb_[:], in0=ex[:], in1=mk[:], op=ALU.mult), reads=[ex, mk], writes=[tb_])
                    for h in range(8):
                        P.mm(pg[:, tt * 128:(tt + 1) * 128], tb_[:, h, :], ident_bf[:], h == 0, h == 7, reads=[tb_, ident_bf], writes=[pg])
                P.add("dve", lambda e, j=j, wt=wt, pg=pg, ab=ab: e.tensor_tensor(out=wt[:, j, :], in0=pg[:], in1=ab[:], op=ALU.mult), reads=[pg, ab], writes=[(wt, j)])
            for db in range(8):
                wb = ws.get()
                vv = wb[:].rearrange("p (c j d) -> p c j d", c=4, j=8)
                for c4i in range(4):
                    dc = db * 4 + c4i
                    po = psO[dc % 2]
                    for j in range(8):
                        P.mm(po[:], vv[:, c4i, j, :], wt[:, j, :], j == 0, j == 7, reads=[wb, wt], writes=[po])
                    if eg == 0:
                        P.add("dve", lambda e, dc=dc, po=po: e.tensor_copy(out=B64[:, dc, :], in_=po[:]), reads=[po], writes=[(B64, dc)])
                    else:
                        P.add("dve", lambda e, dc=dc, po=po: e.tensor_tensor(out=B64[:, dc, :], in0=B64[:, dc, :], in1=po[:], op=ALU.add), reads=[po, (B64, dc)], writes=[(B64, dc)])
        if debug:
            dv = dbg_ffn.t.ap().rearrange("(k p) t -> p k t", p=128)
            for k0 in range(0, 32, 8):
                finals.append(P.dma(dv[:, k0:k0 + 8, t0:t0 + TB], B64[:, k0:k0 + 8, :], reads=[(B64, k) for k in range(k0, k0 + 8)]))
        for dc in range(32):
            h = hp[dc % 2]
            P.dma(h[:], hsc.t.ap()[dc * 128:(dc + 1) * 128, t0:t0 + TB], reads=[(hsc, (dc // 8) * 8)], writes=[h])
            P.add("dve", lambda e, dc=dc, h=h: e.scalar_tensor_tensor(out=B64[:, dc, :], in0=h[:], scalar=DN_ALPHA, in1=B64[:, dc, :], op0=ALU.mult, op1=ALU.add), reads=[h, (B64, dc)], writes=[(B64, dc)])
        layernorm_T(P, B64, 32, _V(lnp_sb, 2), _V(lnp_sb, 3), _V2(const32, ones_ap), psM[0], psM[1], sq, stat)
        ov = outT.t.ap().rearrange("(k p) t -> p k t", p=128)
        for k0 in range(0, 32, 8):
            finals.append(P.dma(ov[:, k0:k0 + 8, t0:t0 + TB], B64[:, k0:k0 + 8, :], reads=[(B64, k) for k in range(k0, k0 + 8)]))
    P.emit(finals)
    return nc, P


class View:
    def __init__(self, buf, ap):
        self.buf = buf
        self.ap = ap
        self.state = buf.state
        self.t = buf.t
        self.name = buf.name

    def __getitem__(self, idx):
        return self.ap[idx]


class _V:
    def __init__(self, buf, i):
        self.buf = buf
        self.i = i
        self.state = buf.state
        self.t = buf.t
        self.name = buf.name

    def __getitem__(self, idx):
        return self.buf.t[idx[0], self.i, idx[1]]


class _V2:
    def __init__(self, buf, ap):
        self.buf = buf
        self.ap = ap
        self.state = buf.state
        self.t = buf.t
        self.name = buf.name

    def __getitem__(self, idx):
        return self.ap


def prep_tail_weights(w_out, ln1g, ln1b, ln2g, ln2b, wq, keys, U, V, glu_w=None, glu_b=None):
    F = w_out.shape[0]
    FC = F // 128
    d = {}
    d["w_out"] = np.ascontiguousarray(w_out.reshape(FC, 128, 32, 128).transpose(2, 1, 0, 3)).reshape(32, 128, FC * 128)
    lnp = np.stack([a.reshape(32, 128).T for a in (ln1g, ln1b, ln2g, ln2b)], axis=1)
    d["lnp"] = np.ascontiguousarray(lnp)
    d["wq"] = np.ascontiguousarray(wq.reshape(32, 128, 16, 128).transpose(2, 1, 0, 3)).reshape(16, 128, 4096)
    d["keysT"] = np.ascontiguousarray(keys.reshape(16, 128, 128).transpose(2, 0, 1))
    d["U"] = np.ascontiguousarray(U.reshape(128, 128, 32, 128).transpose(0, 3, 2, 1)).reshape(128, 128, 4096)
    d["V"] = np.ascontiguousarray(V.reshape(16, 8, 128, 8, 4, 128).transpose(0, 3, 2, 4, 1, 5)).reshape(16, 8, 128, 4096)
    c = np.zeros((128, 2, 128), np.float32)
    c[:, 0, :] = np.eye(128, dtype=np.float32)
    c[:, 1, :] = 1.0
    d["consts"] = c
    if glu_w is not None:
        d["gluw"] = np.ascontiguousarray(glu_w.reshape(8, 128, 8, 128).transpose(2, 1, 0, 3)).reshape(8, 128, 1024)
        d["glub"] = np.ascontiguousarray(glu_b.reshape(8, 128).T)
    return d


SEQ = 2048
TBK = 512
NEG = -30000.0
RMS_EPS = 1e-6


def conv_block(P, ps, xin, hist, fidx, cw, cwidx, y, bias=None):
    P.add("dve", lambda e: e.tensor_copy(out=xin[:, 0:3], in_=hist[:, fidx, :]), reads=[(hist, fidx)], writes=[xin])
    P.add("act", lambda e: e.activation(out=xin[:, 3:3 + TBK], in_=ps[:], func=AF.Copy), reads=[ps], writes=[xin])
    P.add("dve", lambda e: e.tensor_copy(out=hist[:, fidx, :], in_=xin[:, TBK:TBK + 3]), reads=[xin], writes=[(hist, fidx)])
    if bias is None:
        P.add("dve", lambda e: e.tensor_scalar(out=y[:], in0=xin[:, 0:TBK], scalar1=cw[:, cwidx, 0:1], scalar2=None, op0=ALU.mult), reads=[xin, cw], writes=[y])
    else:
        P.add("dve", lambda e: e.tensor_scalar(out=y[:], in0=xin[:, 0:TBK], scalar1=cw[:, cwidx, 0:1], scalar2=bias, op0=ALU.mult, op1=ALU.add), reads=[xin, cw], writes=[y])
    for k in range(1, 4):
        P.add("dve", lambda e, k=k: e.scalar_tensor_tensor(out=y[:], in0=xin[:, k:k + TBK], scalar=cw[:, cwidx, k:k + 1], in1=y[:], op0=ALU.mult, op1=ALU.add), reads=[xin, cw, y], writes=[y])


class StopBuild(Exception):
    pass


def build_even(debug=False, stage=99):
    nc = bass.Bass("TRN2", target_bir_lowering=False)
    P = Prog(nc)
    try:
        _build_even(nc, P, stage)
    except StopBuild:
        pass
    return nc, P


def _build_even(nc, P, stage):
    import os
    stop_head = int(os.environ.get("STOP_HEAD", "0"))
    cur = {"j": 0}

    def stop(k, buf):
        if stage == k and (k in (1, 7) or cur["j"] == stop_head):
            w_ = min(TBK, buf.t.shape[-1]); f = P.dma(rgT.t.ap()[0:buf.t.shape[0], 0:w_], buf[:, 0:w_], reads=[buf])
            P.emit(finals + [f])
            raise StopBuild()
    xT = P.dram("xT", [4096, SEQ], F32, kind="ExternalInput")
    Wg = P.dram("Wg", [48, 128, 4096], F32, kind="ExternalInput")
    Wab = P.dram("Wab", [128, 2, 32, 8], F32, kind="ExternalInput")
    cwg = P.dram("cwg", [128, 24, 4], F32, kind="ExternalInput")
    cwr = P.dram("cwr", [128, 8, 4], F32, kind="ExternalInput")
    rgp = P.dram("rgp", [128, 8, 4], F32, kind="ExternalInput")
    gdp = P.dram("gdp", [8, 2], F32, kind="ExternalInput")
    normw = P.dram("normw", [128, 128], F32, kind="ExternalInput")
    rgw = P.dram("rgw", [128, 8, 2, 128], F32, kind="ExternalInput")
    cst = P.dram("cst", [128, 5, 128], F32, kind="ExternalInput")
    sel = P.dram("sel", [8, 8, 128], F32, kind="ExternalInput")
    rmask = P.dram("rmask", [8, TBK], F32, kind="ExternalInput")
    gdn_out = P.dram("gdn_out", [SEQ, 1024], F32, kind="ExternalOutput")
    rgT = P.dram("rgT", [1024, SEQ], F32, kind="ExternalOutput")

    xb = P.sbuf([128, 32, TBK], BF16, "xb")
    wbs = [P.sbuf([128, 4096], BF16, f"wb{i}") for i in range(4)]
    wab_sb = P.sbuf([128, 2, 32, 8], BF16, "wab_sb")
    cwg_sb = P.sbuf([128, 24, 4], F32, "cwg_sb")
    cwr_sb = P.sbuf([128, 8, 4], F32, "cwr_sb")
    rgp_sb = P.sbuf([128, 8, 4], F32, "rgp_sb")
    gdp_sb = P.sbuf([8, 2], F32, "gdp_sb")
    negA = P.sbuf([8, 1], F32, "negA")
    normw_sb = P.sbuf([128, 128], F32, "normw_sb")
    rgw_sb = P.sbuf([128, 8, 2, 128], F32, "rgw_sb")
    cst_sb = P.sbuf([128, 5, 128], F32, "cst_sb")
    sel_sb = P.sbuf([8, 8, 128], F32, "sel_sb")
    rmask_sb = P.sbuf([8, TBK], F32, "rmask_sb")
    ccol = P.sbuf([128, 8], F32, "ccol")
    hist = P.sbuf([128, 32, 3], F32, "hist")
    hstate = P.sbuf([128, 8], F32, "hstate")
    Sst = P.sbuf([128, 8, 128], F32, "Sst")

    def T512(name):
        return P.sbuf([128, TBK], F32, name)

    xin = [P.sbuf([128, TBK + 3], F32, f"xin{i}") for i in range(2)]
    yv = [T512(f"yv{i}") for i in range(2)]
    qc, kc, vc = T512("qc"), T512("kc"), T512("vc")
    sqb = T512("sqb")
    rsd = T512("rsd")
    qh, kh = T512("qh"), T512("kh")
    zt = T512("zt")
    gcB, EgcB, qdec = T512("gcB"), T512("EgcB"), T512("qdec")
    graw = P.sbuf([8, TBK], F32, "graw")
    gcs = P.sbuf([8, TBK], F32, "gcs")
    bet = P.sbuf([8, TBK], F32, "bet")
    colG = P.sbuf([128, 4, 8], F32, "colG")
    colB = P.sbuf([128, 4, 8], F32, "colB")
    colGL = P.sbuf([128, 4, 8], F32, "colGL")
    colEG = P.sbuf([128, 4, 8], F32, "colEG")
    colBEG = P.sbuf([128, 4, 8], F32, "colBEG")
    colTL = P.sbuf([128, 4, 8], F32, "colTL")
    colNB = P.sbuf([128, 4, 8], F32, "colNB")
    colNG = P.sbuf([128, 4, 8], F32, "colNG")
    kbg, ktl, vbt = T512("kbg"), T512("ktl"), T512("vbt")
    gX, gXT, gam, gamT = T512("gX"), T512("gXT"), T512("gam"), T512("gamT")
    Nm, NTm = T512("Nm"), T512("NTm")
    Pa, PTa, Pb, PTb = T512("Pa"), T512("PTa"), T512("Pb"), T512("PTb")
    TTa, TTb = T512("TTa"), T512("TTb")
    ATm, um, wTm = T512("ATm"), T512("um"), T512("wTm")
    vnew = [P.sbuf([128, 128], F32, f"vnew{i}") for i in range(2)]
    osb = [P.sbuf([128, 128], F32, f"osb{i}") for i in range(2)]
    osq = P.sbuf([128, 128], F32, "osq")
    ms = P.sbuf([128, 4], F32, "ms")
    obuf = T512("obuf")
    xr, rr, ig, aa, bb, hh, gg = T512("xr"), T512("rr"), T512("ig"), T512("aa"), T512("bb"), T512("hh"), T512("gg")

    psA = [P.psum([128, 512], F32, f"psA{i}") for i in range(2)]
    psB = [P.psum([128, 512], F32, f"psB{i}") for i in range(4)]
    psM = [P.psum([128, 512], F32, f"psM{i}") for i in range(2)]

    ident = cst_sb[:, 0, :]
    ones = cst_sb[:, 1, :]
    NEGU = cst_sb[:, 2, :]
    NEGL = cst_sb[:, 3, :]
    Last = cst_sb[:, 4, :]

    P.dma(wab_sb[:], Wab.t.ap(), writes=[wab_sb], q="pool")
    for sb_, dr in [(cwg_sb, cwg), (cwr_sb, cwr), (rgp_sb, rgp), (gdp_sb, gdp), (normw_sb, normw), (rgw_sb, rgw), (cst_sb, cst), (sel_sb, sel), (rmask_sb, rmask)]:
        P.dma(sb_[:], dr.t.ap(), writes=[sb_])
    P.add("dve", lambda e: e.memset(hist[:], 0.0), writes=[hist])
    P.add("dve", lambda e: e.memset(hstate[:], 0.0), writes=[hstate])
    P.add("dve", lambda e: e.memset(Sst[:], 0.0), writes=[Sst])
    P.add("act", lambda e: e.activation(out=negA[:], in_=gdp_sb[:, 0:1], func=AF.Exp), reads=[gdp_sb], writes=[negA])
    P.add("dve", lambda e: e.tensor_scalar(out=negA[:], in0=negA[:], scalar1=-1.0, scalar2=None, op0=ALU.mult), reads=[negA], writes=[negA])
    P.add("act", lambda e: e.activation(out=ccol[:], in_=rgp_sb[:, :, 3], func=AF.Exp, scale=-1.0), reads=[rgp_sb], writes=[ccol])
    P.add("act", lambda e: e.activation(out=ccol[:], in_=ccol[:], func=AF.Ln, bias=1.0), reads=[ccol], writes=[ccol])
    P.add("dve", lambda e: e.tensor_scalar(out=ccol[:], in0=ccol[:], scalar1=-8.0, scalar2=None, op0=ALU.mult), reads=[ccol], writes=[ccol])

    finals = []
    xTv = xT.t.ap().rearrange("(k p) t -> p k t", p=128)
    ncnt = [0]

    def proj(ws, n=TBK):
        wb = ws.get()
        wv = wb[:].rearrange("p (k m) -> p k m", k=32)
        ps = psA[ncnt[0] % 2]
        ncnt[0] += 1
        for kc_ in range(32):
            P.mm(ps[:], wv[:, kc_, :], xb[:, kc_, :], kc_ == 0, kc_ == 31, reads=[wb, xb], writes=[ps])
        return ps

    for tb in range(SEQ // TBK):
        t0 = tb * TBK
        ws = WStream(P, wbs)
        for k0 in range(0, 32, 8):
            P.dma(xb[:, k0:k0 + 8, :], xTv[:, k0:k0 + 8, t0:t0 + TBK], writes=[(xb, k) for k in range(k0, k0 + 8)], q="pool")
        for i in range(48):
            ws.push(Wg.t.ap()[i])
        for ab in range(2):
            ps = psM[ab]
            for kc_ in range(32):
                P.mm(ps[0:8, :], wab_sb[:, ab, kc_, :], xb[:, kc_, :], kc_ == 0, kc_ == 31, reads=[wab_sb, xb], writes=[ps])
        P.add("act", lambda e: e.activation(out=graw[:], in_=psM[0][0:8, :], func=AF.Exp, bias=gdp_sb[:, 1:2]), reads=[psM[0], gdp_sb], writes=[graw])
        P.add("act", lambda e: e.activation(out=graw[:], in_=graw[:], func=AF.Ln, bias=1.0), reads=[graw], writes=[graw])
        P.add("dve", lambda e: e.tensor_scalar(out=graw[:], in0=graw[:], scalar1=negA[:, 0:1], scalar2=None, op0=ALU.mult), reads=[graw, negA], writes=[graw])
        P.add("dve", lambda e: e.tensor_tensor_scan(out=gcs[:], data0=rmask_sb[:], data1=graw[:], initial=0.0, op0=ALU.mult, op1=ALU.add), reads=[rmask_sb, graw], writes=[gcs])
        P.add("act", lambda e: e.activation(out=bet[:], in_=psM[1][0:8, :], func=AF.Sigmoid), reads=[psM[1]], writes=[bet])
        for src, dst, pb in [(gcs, colG, psM[0]), (bet, colB, psM[1])]:
            for c in range(4):
                P.add("pe", lambda e, src=src, pb=pb, c=c: e.transpose(out=pb[:, c * 8:(c + 1) * 8], in_=src[:, c * 128:(c + 1) * 128], identity=cst_sb[0:8, 0, 0:8]), reads=[src, cst_sb], writes=[pb])
            P.add("dve", lambda e, dst=dst, pb=pb: e.tensor_copy(out=dst[:], in_=pb[:, 0:32].rearrange("p (c h) -> p c h", c=4)), reads=[pb], writes=[dst])
        P.mm(psM[0][:, 0:32], Last, colG[:].rearrange("p c h -> p (c h)"), True, True, reads=[cst_sb, colG], writes=[psM[0]])
        P.add("dve", lambda e: e.tensor_copy(out=colGL[:], in_=psM[0][:, 0:32].rearrange("p (c h) -> p c h", c=4)), reads=[psM[0]], writes=[colGL])
        P.add("act", lambda e: e.activation(out=colEG[:], in_=colG[:], func=AF.Exp), reads=[colG], writes=[colEG])
        P.add("dve", lambda e: e.tensor_tensor(out=colBEG[:], in0=colEG[:], in1=colB[:], op=ALU.mult), reads=[colEG, colB], writes=[colBEG])
        P.add("dve", lambda e: e.tensor_tensor(out=colTL[:], in0=colGL[:], in1=colG[:], op=ALU.subtract), reads=[colGL, colG], writes=[colTL])
        P.add("act", lambda e: e.activation(out=colTL[:], in_=colTL[:], func=AF.Exp), reads=[colTL], writes=[colTL])
        P.add("dve", lambda e: e.tensor_scalar(out=colNB[:], in0=colB[:], scalar1=-1.0, scalar2=None, op0=ALU.mult), reads=[colB], writes=[colNB])
        P.add("dve", lambda e: e.tensor_scalar(out=colNG[:], in0=colG[:], scalar1=-1.0, scalar2=None, op0=ALU.mult), reads=[colG], writes=[colNG])

        stop(1, graw)
        for j in range(8):
            cur["j"] = j
            outs = [qc, kc, vc]
            for f in range(3):
                ps = proj(ws)
                xi, y = xin[f % 2], yv[f % 2]
                conv_block(P, ps, xi, hist, j * 3 + f, cwg_sb, j * 3 + f, y)
                P.add("act", lambda e, y=y, o=outs[f]: e.activation(out=o[:], in_=y[:], func=AF.Silu), reads=[y], writes=[outs[f]])
            wb = ws.get()
            wv = wb[:].rearrange("p (k m) -> p k m", k=32)
            pz = psA[ncnt[0] % 2]
            ncnt[0] += 1
            for c in range(4):
                for kc_ in range(32):
                    P.mm(pz[:, c * 128:(c + 1) * 128], xb[:, kc_, c * 128:(c + 1) * 128], wv[:, kc_, :], kc_ == 0, kc_ == 31, reads=[wb, xb], writes=[pz])
            P.add("act", lambda e, pz=pz: e.activation(out=zt[:], in_=pz[:], func=AF.Silu), reads=[pz], writes=[zt])
            stop(2, zt)
            for src, dst, scl, pb in [(qc, qh, 128.0 ** -0.5, psB[0]), (kc, kh, 1.0, psB[1])]:
                P.add("act", lambda e, src=src: e.activation(out=sqb[:], in_=src[:], func=AF.Square), reads=[src], writes=[sqb])
                P.mm(pb[:], ones, sqb[:], True, True, reads=[cst_sb, sqb], writes=[pb])
                P.add("dve", lambda e, pb=pb: e.tensor_scalar(out=rsd[:], in0=pb[:], scalar1=RMS_EPS, scalar2=None, op0=ALU.add), reads=[pb], writes=[rsd])
                P.add("act", lambda e: e.activation(out=rsd[:], in_=rsd[:], func=AF.Sqrt), reads=[rsd], writes=[rsd])
                P.add("dve", lambda e: e.reciprocal(out=rsd[:], in_=rsd[:]), reads=[rsd], writes=[rsd])
                P.add("dve", lambda e, src=src, dst=dst, scl=scl: e.scalar_tensor_tensor(out=dst[:], in0=src[:], scalar=scl, in1=rsd[:], op0=ALU.mult, op1=ALU.mult), reads=[src, rsd], writes=[dst])
            P.mm(psB[2][:], sel_sb[:, j, :], gcs[:], True, True, reads=[sel_sb, gcs], writes=[psB[2]])
            P.add("act", lambda e: e.activation(out=gcB[:], in_=psB[2][:], func=AF.Copy), reads=[psB[2]], writes=[gcB])
            P.add("act", lambda e: e.activation(out=EgcB[:], in_=psB[2][:], func=AF.Exp), reads=[psB[2]], writes=[EgcB])
            P.add("dve", lambda e: e.tensor_tensor(out=qdec[:], in0=qh[:], in1=EgcB[:], op=ALU.mult), reads=[qh, EgcB], writes=[qdec])
            stop(3, qdec)
            for c in range(4):
                P.add("pe", lambda e, c=c: e.transpose(out=psB[0][:, c * 128:(c + 1) * 128], in_=kh[:, c * 128:(c + 1) * 128], identity=ident), reads=[kh, cst_sb], writes=[psB[0]])
            for c in range(4):
                P.add("pe", lambda e, c=c: e.transpose(out=psB[1][:, c * 128:(c + 1) * 128], in_=vc[:, c * 128:(c + 1) * 128], identity=ident), reads=[vc, cst_sb], writes=[psB[1]])
            for c in range(4):
                sl = slice(c * 128, (c + 1) * 128)
                P.add("dve", lambda e, j=j, c=c, sl=sl: e.tensor_scalar(out=kbg[:, sl], in0=psB[0][:, sl], scalar1=colBEG[:, c, j:j + 1], scalar2=None, op0=ALU.mult), reads=[psB[0], colBEG], writes=[kbg])
                P.add("dve", lambda e, j=j, c=c, sl=sl: e.tensor_scalar(out=ktl[:, sl], in0=psB[0][:, sl], scalar1=colTL[:, c, j:j + 1], scalar2=None, op0=ALU.mult), reads=[psB[0], colTL], writes=[ktl])
                P.add("dve", lambda e, j=j, c=c, sl=sl: e.tensor_scalar(out=vbt[:, sl], in0=psB[1][:, sl], scalar1=colB[:, c, j:j + 1], scalar2=None, op0=ALU.mult), reads=[psB[1], colB], writes=[vbt])
            for c in range(4):
                sl = slice(c * 128, (c + 1) * 128)
                P.mm(psB[2][:, sl], kh[:, sl], kh[:, sl], True, True, reads=[kh], writes=[psB[2]])
            for c in range(4):
                sl = slice(c * 128, (c + 1) * 128)
                P.mm(psB[3][:, sl], kh[:, sl], qh[:, sl], True, True, reads=[kh, qh], writes=[psB[3]])
            for c in range(4):
                sl = slice(c * 128, (c + 1) * 128)
                P.add("dve", lambda e, sl=sl: e.scalar_tensor_tensor(out=gX[:, sl], in0=gcB[:, sl], scalar=-1.0, in1=NEGU, op0=ALU.mult, op1=ALU.add), reads=[gcB, cst_sb], writes=[gX])
                P.add("act", lambda e, j=j, c=c, sl=sl: e.activation(out=gam[:, sl], in_=gX[:, sl], func=AF.Exp, bias=colG[:, c, j:j + 1]), reads=[gX, colG], writes=[gam])
                P.add("dve", lambda e, sl=sl: e.tensor_tensor(out=gXT[:, sl], in0=gcB[:, sl], in1=NEGL, op=ALU.add), reads=[gcB, cst_sb], writes=[gXT])
                P.add("act", lambda e, j=j, c=c, sl=sl: e.activation(out=gamT[:, sl], in_=gXT[:, sl], func=AF.Exp, bias=colNG[:, c, j:j + 1]), reads=[gXT, colNG], writes=[gamT])
                P.add("dve", lambda e, j=j, c=c, sl=sl: e.scalar_tensor_tensor(out=Nm[:, sl], in0=psB[2][:, sl], scalar=colNB[:, c, j:j + 1], in1=gam[:, sl], op0=ALU.mult, op1=ALU.mult), reads=[psB[2], colNB, gam], writes=[Nm])
            P.add("dve", lambda e: e.tensor_tensor(out=ATm[:], in0=psB[3][:], in1=gamT[:], op=ALU.mult), reads=[psB[3], gamT], writes=[ATm])
            stop(4, ATm)
            for c in range(4):
                sl = slice(c * 128, (c + 1) * 128)
                P.add("pe", lambda e, sl=sl: e.transpose(out=psB[0][:, sl], in_=Nm[:, sl], identity=ident), reads=[Nm, cst_sb], writes=[psB[0]])
            P.add("act", lambda e: e.activation(out=NTm[:], in_=psB[0][:], func=AF.Copy), reads=[psB[0]], writes=[NTm])
            for c in range(4):
                sl = slice(c * 128, (c + 1) * 128)
                P.add("dve", lambda e, sl=sl: e.tensor_tensor(out=TTa[:, sl], in0=psB[0][:, sl], in1=ident, op=ALU.add), reads=[psB[0], cst_sb], writes=[TTa])
            Pc, PTc, Pn, PTn = Nm, NTm, Pa, PTa
            Tc, Tn = TTa, TTb
            for lvl in range(1, 7):
                for c in range(4):
                    sl = slice(c * 128, (c + 1) * 128)
                    P.mm(psB[1][:, sl], PTc[:, sl], Pc[:, sl], True, True, reads=[PTc, Pc], writes=[psB[1]])
                P.add("act", lambda e, Pn=Pn: e.activation(out=Pn[:], in_=psB[1][:], func=AF.Copy), reads=[psB[1]], writes=[Pn])
                if lvl < 6:
                    for c in range(4):
                        sl = slice(c * 128, (c + 1) * 128)
                        P.mm(psB[2][:, sl], Pc[:, sl], PTc[:, sl], True, True, reads=[PTc, Pc], writes=[psB[2]])
                    P.add("act", lambda e, PTn=PTn: e.activation(out=PTn[:], in_=psB[2][:], func=AF.Copy), reads=[psB[2]], writes=[PTn])
                for c in range(4):
                    sl = slice(c * 128, (c + 1) * 128)
                    P.mm(psB[3][:, sl], Pn[:, sl], Tc[:, sl], True, True, reads=[Pn, Tc], writes=[psB[3]])
                P.add("dve", lambda e, Tn=Tn, Tc=Tc: e.tensor_tensor(out=Tn[:], in0=psB[3][:], in1=Tc[:], op=ALU.add), reads=[psB[3], Tc], writes=[Tn])
                Pc, PTc = Pn, PTn
                Pn, PTn = (Pb, PTb) if Pn is Pa else (Pa, PTa)
                Tc, Tn = Tn, Tc
            TT = Tc
            stop(5, TT)
            for c in range(4):
                sl = slice(c * 128, (c + 1) * 128)
                P.mm(psB[0][:, sl], TT[:, sl], vbt[:, sl], True, True, reads=[TT, vbt], writes=[psB[0]])
            P.add("act", lambda e: e.activation(out=um[:], in_=psB[0][:], func=AF.Copy), reads=[psB[0]], writes=[um])
            for c in range(4):
                sl = slice(c * 128, (c + 1) * 128)
                P.mm(psB[1][:, sl], kbg[:, sl], TT[:, sl], True, True, reads=[TT, kbg], writes=[psB[1]])
            P.add("act", lambda e: e.activation(out=wTm[:], in_=psB[1][:], func=AF.Copy), reads=[psB[1]], writes=[wTm])
            stop(6, wTm)
            Sj = Sst[:, j, :]
            for c in range(4):
                sl = slice(c * 128, (c + 1) * 128)
                vn, ob = vnew[c % 2], osb[c % 2]
                pw, po, pk = psB[2], psB[3], psB[0]
                P.mm(pw[:, 0:128], wTm[:, sl], Sj, True, True, reads=[wTm, (Sst, j)], writes=[pw])
                P.add("dve", lambda e, sl=sl, vn=vn, pw=pw: e.tensor_tensor(out=vn[:], in0=um[:, sl], in1=pw[:, 0:128], op=ALU.subtract), reads=[um, pw], writes=[vn])
                P.mm(po[:, 0:128], qdec[:, sl], Sj, True, False, reads=[qdec, (Sst, j)], writes=[po])
                P.mm(po[:, 0:128], ATm[:, sl], vn[:], False, True, reads=[ATm, vn], writes=[po])
                P.mm(pk[:, 0:128], ktl[:, sl], vn[:], True, True, reads=[ktl, vn], writes=[pk])
                P.add("dve", lambda e, j=j, c=c, pk=pk: e.scalar_tensor_tensor(out=Sst[:, j, :], in0=Sst[:, j, :], scalar=EgcB[:, c * 128 + 127:c * 128 + 128], in1=pk[:, 0:128], op0=ALU.mult, op1=ALU.add), reads=[(Sst, j), EgcB, pk], writes=[(Sst, j)])
                P.add("act", lambda e, c=c, po=po: e.activation(out=osq[:], in_=po[:, 0:128], func=AF.Square), reads=[po], writes=[osq])
                P.add("dve", lambda e, c=c: e.tensor_reduce(out=ms[:, c:c + 1], in_=osq[:], axis=AX.X, op=ALU.add), reads=[osq], writes=[ms])
                P.add("dve", lambda e, c=c: e.tensor_scalar(out=ms[:, c:c + 1], in0=ms[:, c:c + 1], scalar1=1.0 / 128, scalar2=RMS_EPS, op0=ALU.mult, op1=ALU.add), reads=[ms], writes=[ms])
                P.add("act", lambda e, c=c: e.activation(out=ms[:, c:c + 1], in_=ms[:, c:c + 1], func=AF.Sqrt), reads=[ms], writes=[ms])
                P.add("dve", lambda e, c=c: e.reciprocal(out=ms[:, c:c + 1], in_=ms[:, c:c + 1]), reads=[ms], writes=[ms])
                if c == 0:
                    stop(8, vn)
                    stop(9, osq)
                    stop(10, ms)
                P.add("dve", lambda e, c=c, po=po, ob=ob: e.scalar_tensor_tensor(out=ob[:], in0=po[:, 0:128], scalar=ms[:, c:c + 1], in1=normw_sb[:], op0=ALU.mult, op1=ALU.mult), reads=[po, ms, normw_sb], writes=[ob])
                if c == 0:
                    stop(11, ob)
                P.add("dve", lambda e, sl=sl, ob=ob: e.tensor_tensor(out=obuf[:, sl], in0=ob[:], in1=zt[:, sl], op=ALU.mult), reads=[ob, zt], writes=[obuf])
            ov = gdn_out.t.ap()[t0:t0 + TBK, j * 128:(j + 1) * 128].rearrange("(c p) d -> p c d", p=128)
            finals.append(P.dma(ov, obuf[:].rearrange("p (c d) -> p c d", c=4), reads=[obuf]))

        stop(7, obuf)
        for n in range(8):
            ps = proj(ws)
            xi = xin[n % 2]
            conv_block(P, ps, xi, hist, 24 + n, cwr_sb, n, xr, bias=rgp_sb[:, n, 0:1])
            pg = proj(ws)
            P.add("act", lambda e, pg=pg: e.activation(out=gg[:], in_=pg[:], func=AF.Gelu_apprx_tanh), reads=[pg], writes=[gg])
            P.mm(psB[0][:], rgw_sb[:, n, 0, :], xr[:], True, True, reads=[rgw_sb, xr], writes=[psB[0]])
            P.mm(psB[1][:], rgw_sb[:, n, 1, :], xr[:], True, True, reads=[rgw_sb, xr], writes=[psB[1]])
            P.add("act", lambda e, n=n: e.activation(out=rr[:], in_=psB[0][:], func=AF.Sigmoid, bias=rgp_sb[:, n, 1:2]), reads=[psB[0], rgp_sb], writes=[rr])
            P.add("act", lambda e, n=n: e.activation(out=ig[:], in_=psB[1][:], func=AF.Sigmoid, bias=rgp_sb[:, n, 2:3]), reads=[psB[1], rgp_sb], writes=[ig])
            P.add("act", lambda e, n=n: e.activation(out=aa[:], in_=rr[:], func=AF.Exp, scale=ccol[:, n:n + 1]), reads=[rr, ccol], writes=[aa])
            P.add("dve", lambda e: e.tensor_tensor(out=bb[:], in0=aa[:], in1=aa[:], op=ALU.mult), reads=[aa], writes=[bb])
            P.add("dve", lambda e: e.tensor_scalar(out=bb[:], in0=bb[:], scalar1=-1.0, scalar2=1.0, op0=ALU.mult, op1=ALU.add), reads=[bb], writes=[bb])
            P.add("dve", lambda e: e.tensor_scalar(out=bb[:], in0=bb[:], scalar1=0.0, scalar2=None, op0=ALU.max), reads=[bb], writes=[bb])
            P.add("act", lambda e: e.activation(out=bb[:], in_=bb[:], func=AF.Sqrt), reads=[bb], writes=[bb])
            P.add("dve", lambda e: e.tensor_tensor(out=ig[:], in0=ig[:], in1=xr[:], op=ALU.mult), reads=[ig, xr], writes=[ig])
            P.add("dve", lambda e: e.tensor_tensor(out=bb[:], in0=bb[:], in1=ig[:], op=ALU.mult), reads=[bb, ig], writes=[bb])
            P.add("dve", lambda e, n=n: e.tensor_tensor_scan(out=hh[:], data0=aa[:], data1=bb[:], initial=hstate[:, n:n + 1], op0=ALU.mult, op1=ALU.add), reads=[aa, bb, hstate], writes=[hh])
            P.add("dve", lambda e, n=n: e.tensor_copy(out=hstate[:, n:n + 1], in_=hh[:, TBK - 1:TBK]), reads=[hh], writes=[hstate])
            P.add("dve", lambda e: e.tensor_tensor(out=gg[:], in0=gg[:], in1=hh[:], op=ALU.mult), reads=[gg, hh], writes=[gg])
            finals.append(P.dma(rgT.t.ap()[n * 128:(n + 1) * 128, t0:t0 + TBK], gg[:], reads=[gg]))
    P.emit(finals)


def prep_even(inp, b, half):
    x = inp["x"]
    W = inp["w_in_e"][0]
    d = {}
    d["xT"] = np.ascontiguousarray(x[b].T)
    cols = []
    for j in range(8):
        h = half * 8 + j
        for base in (0, 2048, 4096, 6144):
            cols.append(base + h * 128)
    for j in range(8):
        n = half * 8 + j
        cols.append(8224 + n * 128)
        cols.append(10272 + n * 128)
    Wg = np.stack([W[:, c:c + 128] for c in cols])
    d["Wg"] = np.ascontiguousarray(Wg.reshape(48, 32, 128, 128).transpose(0, 2, 1, 3)).reshape(48, 128, 4096)
    wa = W[:, 8192 + half * 8:8192 + half * 8 + 8]
    wb = W[:, 8208 + half * 8:8208 + half * 8 + 8]
    Wab = np.stack([wa, wb])
    d["Wab"] = np.ascontiguousarray(Wab.reshape(2, 32, 128, 8).transpose(2, 0, 1, 3))
    gcw = inp["gdn_conv_w"][0]
    cwg = np.zeros((128, 24, 4), np.float32)
    for j in range(8):
        h = half * 8 + j
        for f, base in enumerate((0, 2048, 4096)):
            cwg[:, j * 3 + f, :] = gcw[:, base + h * 128:base + (h + 1) * 128].T
    d["cwg"] = cwg
    rcw = inp["rg_conv_w"][0]
    sl = slice(half * 1024, half * 1024 + 1024)
    d["cwr"] = np.ascontiguousarray(rcw[:, sl].reshape(4, 8, 128).transpose(2, 1, 0))
    rgp = np.stack([inp["rg_conv_b"][0][sl], inp["rg_ba"][0][sl], inp["rg_bx"][0][sl], inp["rg_lambda"][0][sl]], axis=-1)
    d["rgp"] = np.ascontiguousarray(rgp.reshape(8, 128, 4).transpose(1, 0, 2))
    hs = slice(half * 8, half * 8 + 8)
    d["gdp"] = np.ascontiguousarray(np.stack([inp["gdn_A_log"][0][hs], inp["gdn_dt_bias"][0][hs]], axis=-1))
    d["normw"] = np.ascontiguousarray(np.broadcast_to(inp["gdn_norm_w"][0][None, :], (128, 128)))
    rgw = np.stack([inp["rg_wa"][0][hs], inp["rg_wx"][0][hs]], axis=1)
    d["rgw"] = np.ascontiguousarray(rgw.transpose(2, 0, 1, 3))
    cst = np.zeros((128, 5, 128), np.float32)
    ii, jj = np.meshgrid(np.arange(128), np.arange(128), indexing="ij")
    cst[:, 0, :] = np.eye(128)
    cst[:, 1, :] = 1.0
    cst[:, 2, :] = np.where(jj >= ii, NEG, 0.0)
    cst[:, 3, :] = np.where(jj < ii, NEG, 0.0)
    cst[127, 4, :] = 1.0
    d["cst"] = cst
    sel = np.zeros((8, 8, 128), np.float32)
    for h in range(8):
        sel[h, h, :] = 1.0
    d["sel"] = sel
    rm = np.ones((8, TBK), np.float32)
    rm[:, ::128] = 0.0
    d["rmask"] = rm
    return d


SEQ = 2048
TBK = 512
PI = math.pi


def build_odd(stage=99):
    nc = bass.Bass("TRN2", target_bir_lowering=False)
    P = Prog(nc)
    xT = P.dram("xT", [4096, SEQ], F32, kind="ExternalInput")
    Wg = P.dram("Wg", [28, 128, 4096], F32, kind="ExternalInput")
    masks = P.dram("masks", [128, 4, 512], F32, kind="ExternalInput")
    cst = P.dram("cst", [128, 3, 128], F32, kind="ExternalInput")
    s5p = P.dram("s5p", [128, 3, 16], F32, kind="ExternalInput")
    s5B = P.dram("s5B", [128, 2, 16, 16], F32, kind="ExternalInput")
    s5C = P.dram("s5C", [128, 2, 16, 16], F32, kind="ExternalInput")
    s5D = P.dram("s5D", [128, 4], F32, kind="ExternalInput")
    glm = P.dram("glm", [128, 2], F32, kind="ExternalInput")
    sbT = P.dram("sbT", [1024, SEQ], F32, kind="ExternalOutput")
    ygT = P.dram("ygT", [512, SEQ], F32, kind="ExternalOutput")
    qs = P.dram("qs", [8, 128, SEQ], F32)
    ks = P.dram("ks", [8, 128, SEQ], F32)
    vs = P.dram("vs", [8, 128, 16, 128], F32)
    us = P.dram("us", [4, 128, SEQ], F32)

    finals = []
    xb = P.sbuf([128, 32, TBK], BF16, "xb")
    wbs = [P.sbuf([128, 4096], BF16, f"wb{i}") for i in range(4)]
    stg = [P.sbuf([128, TBK], F32, f"stg{i}") for i in range(3)]
    cst_sb = P.sbuf([128, 3, 128], F32, "cst_sb")
    masks_sb = P.sbuf([128, 4, 512], F32, "masks_sb")
    psA = [P.psum([128, 512], F32, f"psA{i}") for i in range(8)]
    ident, ones, UT = cst_sb[:, 0, :], cst_sb[:, 1, :], cst_sb[:, 2, :]
    P.dma(cst_sb[:], cst.t.ap(), writes=[cst_sb])
    P.dma(masks_sb[:], masks.t.ap(), writes=[masks_sb])
    xTv = xT.t.ap().rearrange("(k p) t -> p k t", p=128)
    cnt = 0
    for tb in range(SEQ // TBK):
        t0 = tb * TBK
        ws = WStream(P, wbs)
        for k0 in range(0, 32, 8):
            P.dma(xb[:, k0:k0 + 8, :], xTv[:, k0:k0 + 8, t0:t0 + TBK], writes=[(xb, k) for k in range(k0, k0 + 8)], q="pool")
        for i in range(28):
            ws.push(Wg.t.ap()[i])
        for i in range(28):
            wb = ws.get()
            wv = wb[:].rearrange("p (k m) -> p k m", k=32)
            ps = psA[cnt % 2]
            sg = stg[cnt % 3]
            cnt += 1
            tok_major = (i < 24 and i % 3 == 2)
            if not tok_major:
                for kc in range(32):
                    P.mm(ps[:], wv[:, kc, :], xb[:, kc, :], kc == 0, kc == 31, reads=[wb, xb], writes=[ps])
            else:
                for c in range(4):
                    for kc in range(32):
                        P.mm(ps[:, c * 128:(c + 1) * 128], xb[:, kc, c * 128:(c + 1) * 128], wv[:, kc, :], kc == 0, kc == 31, reads=[wb, xb], writes=[ps])
            P.add("act", lambda e, ps=ps, sg=sg: e.activation(out=sg[:], in_=ps[:], func=AF.Copy), reads=[ps], writes=[sg])
            if i < 24:
                h, f = i // 3, i % 3
                if f == 0:
                    P.dma(qs.t.ap()[h, :, t0:t0 + TBK], sg[:], reads=[sg], writes=[(qs, h)])
                elif f == 1:
                    P.dma(ks.t.ap()[h, :, t0:t0 + TBK], sg[:], reads=[sg], writes=[(ks, h)])
                else:
                    P.dma(vs.t.ap()[h, :, tb * 4:tb * 4 + 4, :], sg[:].rearrange("p (c d) -> p c d", c=4), reads=[sg], writes=[(vs, h)])
            else:
                P.dma(us.t.ap()[i - 24, :, t0:t0 + TBK], sg[:], reads=[sg], writes=[(us, i - 24)])

    G = [P.sbuf([128, SEQ], F32, f"G{i}") for i in range(7)]
    qT, kT, Acc = G[0], G[1], G[3]
    vv = View(G[2], G[2][:].rearrange("p (s d) -> p s d", s=16))
    e1 = [P.sbuf([128, 512], F32, f"e1_{i}") for i in range(2)]
    sp = [P.sbuf([128, 512], F32, f"sp_{i}") for i in range(2)]
    Lb = [P.sbuf([128, 512], F32, f"Lb_{i}") for i in range(2)]
    t1 = [P.sbuf([128, 512], F32, f"t1_{i}") for i in range(2)]
    wts = [P.sbuf([128, 512], F32, f"wts_{i}") for i in range(2)]
    ob = [P.sbuf([128, 512], F32, f"ob_{i}") for i in range(2)]
    psO = psA[0:4]
    psZ = psA[4:6]
    psT = psA[6:8]
    scale = 128.0 ** -0.5
    it = 0
    nheads = 8 if stage >= 2 else 0
    for h in range(nheads):
        P.dma(qT[:], qs.t.ap()[h], reads=[(qs, h)], writes=[qT])
        P.dma(kT[:], ks.t.ap()[h], reads=[(ks, h)], writes=[kT])
        P.dma(vv[:], vs.t.ap()[h], reads=[(vs, h)], writes=[vv])
        P.add("dve", lambda e: e.memset(Acc[:], 0.0), writes=[Acc])
        for S in range(15, -1, -1):
            for p in range(3, S // 4 - 1, -1):
                k = it % 2
                it += 1
                pz, pt = psZ[k], psT[k]
                cs = slice(p * 512, (p + 1) * 512)
                diag = (S // 4 == p)
                P.mm(pz[:], kT[:, S * 128:(S + 1) * 128], qT[:, cs], True, True, reads=[kT, qT], writes=[pz])
                P.add("act", lambda e, k=k, pz=pz: e.activation(out=e1[k][:], in_=pz[:], func=AF.Exp, scale=scale), reads=[pz], writes=[e1[k]])
                P.add("act", lambda e, k=k: e.activation(out=sp[k][:], in_=e1[k][:], func=AF.Ln, bias=1.0), reads=[e1[k]], writes=[sp[k]])
                if diag:
                    P.add("dve", lambda e, k=k, S=S: e.scalar_tensor_tensor(out=Lb[k][:], in0=sp[k][:], scalar=-1.0, in1=masks_sb[:, S % 4, :], op0=ALU.mult, op1=ALU.mult), reads=[sp[k], masks_sb], writes=[Lb[k]])
                else:
                    P.add("dve", lambda e, k=k: e.tensor_scalar(out=Lb[k][:], in0=sp[k][:], scalar1=-1.0, scalar2=None, op0=ALU.mult), reads=[sp[k]], writes=[Lb[k]])
                P.mm(pt[:], ones, Acc[:, cs], True, False, reads=[cst_sb, (Acc, p)], writes=[pt])
                P.mm(pt[:], UT, Lb[k][:], False, True, reads=[cst_sb, Lb[k]], writes=[pt])
                P.add("dve", lambda e, k=k, pz=pz: e.scalar_tensor_tensor(out=t1[k][:], in0=pz[:], scalar=scale, in1=sp[k][:], op0=ALU.mult, op1=ALU.subtract), reads=[pz, sp[k]], writes=[t1[k]])
                P.add("dve", lambda e, k=k, pt=pt: e.tensor_tensor(out=t1[k][:], in0=t1[k][:], in1=pt[:], op=ALU.add), reads=[t1[k], pt], writes=[t1[k]])
                P.add("act", lambda e, k=k: e.activation(out=wts[k][:], in_=t1[k][:], func=AF.Exp), reads=[t1[k]], writes=[wts[k]])
                if diag:
                    P.add("dve", lambda e, k=k, S=S: e.tensor_tensor(out=wts[k][:], in0=wts[k][:], in1=masks_sb[:, S % 4, :], op=ALU.mult), reads=[wts[k], masks_sb], writes=[wts[k]])
                P.add("dve", lambda e, k=k, cs=cs: e.tensor_tensor(out=Acc[:, cs], in0=Acc[:, cs], in1=Lb[k][:], op=ALU.add), reads=[(Acc, p), Lb[k]], writes=[(Acc, p)])
                first = (S == min(15, 4 * p + 3))
                P.mm(psO[p][:], vv[:, S, :], wts[k][:], first, S == 0, reads=[vv, wts[k]], writes=[psO[p]])
        for p in range(4):
            o = ob[p % 2]
            P.add("act", lambda e, o=o, p=p: e.activation(out=o[:], in_=psO[p][:], func=AF.Copy), reads=[psO[p]], writes=[o])
            finals.append(P.dma(sbT.t.ap()[h * 128:(h + 1) * 128, p * 512:(p + 1) * 512], o[:], reads=[o]))

    if stage >= 3:
        s5p_sb = P.sbuf([128, 3, 16], F32, "s5p_sb")
        s5B_sb = P.sbuf([128, 2, 16, 16], F32, "s5B_sb")
        s5C_sb = P.sbuf([128, 2, 16, 16], F32, "s5C_sb")
        s5D_sb = P.sbuf([128, 4], F32, "s5D_sb")
        glm_sb = P.sbuf([128, 2], F32, "glm_sb")
        for sb_, dr in [(s5p_sb, s5p), (s5B_sb, s5B), (s5C_sb, s5C), (s5D_sb, s5D), (glm_sb, glm)]:
            P.dma(sb_[:], dr.t.ap(), writes=[sb_])

        def S16(name):
            return P.sbuf([128, 16], F32, name)

        dt, lr, ang, mag, ta, tb_, sn, cs_, den, nr, crr, cii = [S16(n) for n in ["dt", "lr", "ang", "mag", "ta", "tb_", "sn", "cs_", "den", "nr", "crr", "cii"]]
        dve = lambda fn, reads, writes: P.add("dve", fn, reads=reads, writes=writes)
        P.add("act", lambda e: e.activation(out=dt[:], in_=s5p_sb[:, 2, :], func=AF.Exp), reads=[s5p_sb], writes=[dt])
        dve(lambda e: e.tensor_scalar(out=lr[:], in0=s5p_sb[:, 0, :], scalar1=-1e-4, scalar2=None, op0=ALU.min), [s5p_sb], [lr])
        dve(lambda e: e.tensor_tensor(out=ang[:], in0=s5p_sb[:, 1, :], in1=dt[:], op=ALU.mult), [s5p_sb, dt], [ang])
        dve(lambda e: e.tensor_tensor(out=mag[:], in0=lr[:], in1=dt[:], op=ALU.mult), [lr, dt], [mag])
        P.add("act", lambda e: e.activation(out=mag[:], in_=mag[:], func=AF.Exp), reads=[mag], writes=[mag])

        def sin_of(dst, shift):
            dve(lambda e: e.tensor_scalar(out=ta[:], in0=ang[:], scalar1=shift, scalar2=None, op0=ALU.add), [ang], [ta])
            dve(lambda e: e.tensor_copy(out=dst[:], in_=ta[:]), [ta], [dst])
            for m in range(8):
                dve(lambda e, m=m: e.tensor_scalar(out=tb_[:], in0=ta[:], scalar1=(2 * m + 1) * PI, scalar2=-2 * PI, op0=ALU.is_gt, op1=ALU.mult), [ta], [tb_])
                dve(lambda e: e.tensor_tensor(out=dst[:], in0=dst[:], in1=tb_[:], op=ALU.add), [dst, tb_], [dst])
            P.add("act", lambda e: e.activation(out=dst[:], in_=dst[:], func=AF.Sin), reads=[dst], writes=[dst])

        sin_of(sn, 0.0)
        sin_of(cs_, PI / 2)
        abre, abim = S16("abre"), S16("abim")
        dve(lambda e: e.tensor_tensor(out=abre[:], in0=mag[:], in1=cs_[:], op=ALU.mult), [mag, cs_], [abre])
        dve(lambda e: e.tensor_tensor(out=abim[:], in0=mag[:], in1=sn[:], op=ALU.mult), [mag, sn], [abim])
        li = s5p_sb[:, 1, :]
        dve(lambda e: e.tensor_tensor(out=den[:], in0=lr[:], in1=lr[:], op=ALU.mult), [lr], [den])
        dve(lambda e: e.tensor_tensor(out=ta[:], in0=li, in1=li, op=ALU.mult), [s5p_sb], [ta])
        dve(lambda e: e.tensor_tensor(out=den[:], in0=den[:], in1=ta[:], op=ALU.add), [den, ta], [den])
        dve(lambda e: e.reciprocal(out=den[:], in_=den[:]), [den], [den])
        dve(lambda e: e.tensor_scalar(out=nr[:], in0=abre[:], scalar1=-1.0, scalar2=None, op0=ALU.add), [abre], [nr])
        dve(lambda e: e.tensor_tensor(out=ta[:], in0=nr[:], in1=lr[:], op=ALU.mult), [nr, lr], [ta])
        dve(lambda e: e.tensor_tensor(out=tb_[:], in0=abim[:], in1=li, op=ALU.mult), [abim, s5p_sb], [tb_])
        dve(lambda e: e.tensor_tensor(out=ta[:], in0=ta[:], in1=tb_[:], op=ALU.add), [ta, tb_], [ta])
        dve(lambda e: e.tensor_tensor(out=crr[:], in0=ta[:], in1=den[:], op=ALU.mult), [ta, den], [crr])
        dve(lambda e: e.tensor_tensor(out=ta[:], in0=abim[:], in1=lr[:], op=ALU.mult), [abim, lr], [ta])
        dve(lambda e: e.tensor_tensor(out=tb_[:], in0=nr[:], in1=li, op=ALU.mult), [nr, s5p_sb], [tb_])
        dve(lambda e: e.tensor_tensor(out=ta[:], in0=ta[:], in1=tb_[:], op=ALU.subtract), [ta, tb_], [ta])
        dve(lambda e: e.tensor_tensor(out=cii[:], in0=ta[:], in1=den[:], op=ALU.mult), [ta, den], [cii])
        bbre = P.sbuf([128, 16, 16], F32, "bbre")
        bbim = P.sbuf([128, 16, 16], F32, "bbim")
        tq = P.sbuf([128, 16, 16], F32, "tq")
        crb = crr[:].unsqueeze(2).to_broadcast([128, 16, 16])
        cib = cii[:].unsqueeze(2).to_broadcast([128, 16, 16])
        dve(lambda e: e.tensor_tensor(out=bbre[:], in0=s5B_sb[:, 0], in1=crb, op=ALU.mult), [s5B_sb, crr], [bbre])
        dve(lambda e: e.tensor_tensor(out=tq[:], in0=s5B_sb[:, 1], in1=cib, op=ALU.mult), [s5B_sb, cii], [tq])
        dve(lambda e: e.tensor_tensor(out=bbre[:], in0=bbre[:], in1=tq[:], op=ALU.subtract), [bbre, tq], [bbre])
        dve(lambda e: e.tensor_tensor(out=bbim[:], in0=s5B_sb[:, 1], in1=crb, op=ALU.mult), [s5B_sb, crr], [bbim])
        dve(lambda e: e.tensor_tensor(out=tq[:], in0=s5B_sb[:, 0], in1=cib, op=ALU.mult), [s5B_sb, cii], [tq])
        dve(lambda e: e.tensor_tensor(out=bbim[:], in0=bbim[:], in1=tq[:], op=ALU.add), [bbim, tq], [bbim])
        pc = P.sbuf([128, 11, 16], F32, "pc")
        ps_ = P.sbuf([128, 11, 16], F32, "ps_")
        dve(lambda e: e.tensor_copy(out=pc[:, 0, :], in_=cs_[:]), [cs_], [pc])
        dve(lambda e: e.tensor_copy(out=ps_[:, 0, :], in_=sn[:]), [sn], [ps_])
        for k in range(1, 11):
            dve(lambda e, k=k: e.tensor_tensor(out=ta[:], in0=pc[:, k - 1, :], in1=pc[:, k - 1, :], op=ALU.mult), [pc], [ta])
            dve(lambda e, k=k: e.tensor_tensor(out=tb_[:], in0=ps_[:, k - 1, :], in1=ps_[:, k - 1, :], op=ALU.mult), [ps_], [tb_])
            dve(lambda e, k=k: e.tensor_tensor(out=pc[:, k, :], in0=ta[:], in1=tb_[:], op=ALU.subtract), [ta, tb_], [pc])
            dve(lambda e, k=k: e.tensor_tensor(out=ta[:], in0=pc[:, k - 1, :], in1=ps_[:, k - 1, :], op=ALU.mult), [pc, ps_], [ta])
            dve(lambda e, k=k: e.tensor_scalar(out=ps_[:, k, :], in0=ta[:], scalar1=2.0, scalar2=None, op0=ALU.mult), [ta], [ps_])

        Ec, Es, btr, bti, hre, him, uch = G
        rhoT = P.sbuf([128, 512], F32, "rhoT")
        nps = P.sbuf([128, 11, 16], F32, "nps")
        dve(lambda e: e.tensor_scalar(out=nps[:], in0=ps_[:], scalar1=-1.0, scalar2=None, op0=ALU.mult), [ps_], [nps])
        X4 = [P.sbuf([128, 128], F32, f"X4_{i}") for i in range(2)]
        Zre = P.sbuf([128, 128], F32, "Zre")
        Zim = P.sbuf([128, 128], F32, "Zim")
        CX = [P.sbuf([128, 128], F32, f"CX_{i}") for i in range(2)]
        tmp5 = [P.sbuf([128, 512], F32, f"tmp5_{i}") for i in range(2)]
        ygb = [P.sbuf([128, 512], F32, f"ygb_{i}") for i in range(2)]
        psY = psA[0:4]
        psBu = psA[4:8]
        for q in range(16):
            uc, pr = q // 4, q % 4
            if pr == 0:
                P.dma(uch[:], us.t.ap()[uc], reads=[(us, uc)], writes=[uch])
            dve(lambda e: e.memset(Ec[:, 0:1], 1.0), [], [Ec])
            dve(lambda e: e.memset(Es[:, 0:1], 0.0), [], [Es])
            for k in range(11):
                L = 1 << k
                c_, s_, ns_ = pc[:, k, q:q + 1], ps_[:, k, q:q + 1], nps[:, k, q:q + 1]
                dve(lambda e, L=L, c_=c_: e.tensor_scalar(out=Ec[:, L:2 * L], in0=Ec[:, 0:L], scalar1=c_, scalar2=None, op0=ALU.mult), [Ec, pc], [Ec])
                dve(lambda e, L=L, ns_=ns_: e.scalar_tensor_tensor(out=Ec[:, L:2 * L], in0=Es[:, 0:L], scalar=ns_, in1=Ec[:, L:2 * L], op0=ALU.mult, op1=ALU.add), [Ec, Es, nps], [Ec])
                dve(lambda e, L=L, c_=c_: e.tensor_scalar(out=Es[:, L:2 * L], in0=Es[:, 0:L], scalar1=c_, scalar2=None, op0=ALU.mult), [Es, pc], [Es])
                dve(lambda e, L=L, s_=s_: e.scalar_tensor_tensor(out=Es[:, L:2 * L], in0=Ec[:, 0:L], scalar=s_, in1=Es[:, L:2 * L], op0=ALU.mult, op1=ALU.add), [Ec, Es, ps_], [Es])
            for ri, (bb_, Z) in enumerate([(bbre, Zre), (bbim, Zim)]):
                x4 = X4[ri]
                dve(lambda e, x4=x4: e.memset(x4[:], 0.0), [], [x4])
                for gl in range(2):
                    o0 = pr * 32 + gl * 16
                    dve(lambda e, x4=x4, bb_=bb_, gl=gl, o0=o0, q=q: e.tensor_scalar(out=x4[:, o0:o0 + 16], in0=bb_[:, q, :], scalar1=glm_sb[:, gl:gl + 1], scalar2=None, op0=ALU.mult), [bb_, glm_sb, x4], [x4])
                pb = psBu[ri]
                P.add("pe", lambda e, x4=x4, pb=pb: e.transpose(out=pb[:, 0:128], in_=x4[:], identity=ident), reads=[x4, cst_sb], writes=[pb])
                P.add("act", lambda e, Z=Z, pb=pb: e.activation(out=Z[:], in_=pb[:, 0:128], func=AF.Copy), reads=[pb], writes=[Z])
            for ri in range(2):
                cx = CX[ri]
                dve(lambda e, cx=cx: e.memset(cx[:], 0.0), [], [cx])
                for gl in range(2):
                    o0 = pr * 32 + gl * 16
                    sgn = 1.0 if ri == 0 else -1.0
                    dve(lambda e, cx=cx, ri=ri, gl=gl, o0=o0, q=q, sgn=sgn: e.tensor_scalar(out=cx[:, o0:o0 + 16], in0=s5C_sb[:, ri, q, :], scalar1=glm_sb[:, gl:gl + 1], scalar2=sgn, op0=ALU.mult, op1=ALU.mult), [s5C_sb, glm_sb, cx], [cx])
            for pp in range(4):
                cs = slice(pp * 512, (pp + 1) * 512)
                pre, pim = psBu[(pp % 2) * 2], psBu[(pp % 2) * 2 + 1]
                P.mm(pre[:], Zre[:], uch[:, cs], True, True, reads=[Zre, uch], writes=[pre])
                P.mm(pim[:], Zim[:], uch[:, cs], True, True, reads=[Zim, uch], writes=[pim])
                dve(lambda e, cs=cs, pre=pre: e.tensor_tensor(out=btr[:, cs], in0=pre[:], in1=Ec[:, cs], op=ALU.mult), [pre, Ec], [(btr, pp)])
                dve(lambda e, cs=cs, pim=pim: e.tensor_tensor(out=tmp5[1][:], in0=pim[:], in1=Es[:, cs], op=ALU.mult), [pim, Es], [tmp5[1]])
                dve(lambda e, cs=cs: e.tensor_tensor(out=btr[:, cs], in0=btr[:, cs], in1=tmp5[1][:], op=ALU.add), [(btr, pp), tmp5[1]], [(btr, pp)])
                dve(lambda e, cs=cs, pim=pim: e.tensor_tensor(out=bti[:, cs], in0=pim[:], in1=Ec[:, cs], op=ALU.mult), [pim, Ec], [(bti, pp)])
                dve(lambda e, cs=cs, pre=pre: e.tensor_tensor(out=tmp5[1][:], in0=pre[:], in1=Es[:, cs], op=ALU.mult), [pre, Es], [tmp5[1]])
                dve(lambda e, cs=cs: e.tensor_tensor(out=bti[:, cs], in0=bti[:, cs], in1=tmp5[1][:], op=ALU.subtract), [(bti, pp), tmp5[1]], [(bti, pp)])
            dve(lambda e, q=q: e.tensor_scalar(out=rhoT[:], in0=Ec[:, 0:1].to_broadcast([128, 512]), scalar1=mag[:, q:q + 1], scalar2=None, op0=ALU.mult), [Ec, mag], [rhoT])
            for src, dst in [(btr, hre), (bti, him)]:
                for pp in range(4):
                    cs = slice(pp * 512, (pp + 1) * 512)
                    init = 0.0 if pp == 0 else dst[:, pp * 512 - 1:pp * 512]
                    dve(lambda e, src=src, dst=dst, cs=cs, init=init: e.tensor_tensor_scan(out=dst[:, cs], data0=rhoT[:], data1=src[:, cs], initial=init, op0=ALU.mult, op1=ALU.add), [rhoT, src, dst], [dst])
            dve(lambda e: e.tensor_tensor(out=btr[:], in0=hre[:], in1=Ec[:], op=ALU.mult), [hre, Ec], [btr])
            dve(lambda e: e.tensor_tensor(out=bti[:], in0=him[:], in1=Es[:], op=ALU.mult), [him, Es], [bti])
            dve(lambda e: e.tensor_tensor(out=btr[:], in0=btr[:], in1=bti[:], op=ALU.subtract), [btr, bti], [btr])
            dve(lambda e: e.tensor_tensor(out=bti[:], in0=hre[:], in1=Es[:], op=ALU.mult), [hre, Es], [bti])
            dve(lambda e: e.tensor_tensor(out=hre[:], in0=him[:], in1=Ec[:], op=ALU.mult), [him, Ec], [hre])
            dve(lambda e: e.tensor_tensor(out=bti[:], in0=bti[:], in1=hre[:], op=ALU.add), [bti, hre], [bti])
            for pp in range(4):
                cs = slice(pp * 512, (pp + 1) * 512)
                P.mm(psY[pp][:], CX[0][:], btr[:, cs], pr == 0, False, reads=[CX[0], btr], writes=[psY[pp]])
                P.mm(psY[pp][:], CX[1][:], bti[:, cs], False, pr == 3, reads=[CX[1], bti], writes=[psY[pp]])
            if pr == 3:
                for pp in range(4):
                    cs = slice(pp * 512, (pp + 1) * 512)
                    yb = ygb[pp % 2]
                    dve(lambda e, cs=cs, pp=pp, uc=uc: e.scalar_tensor_tensor(out=tmp5[1][:], in0=uch[:, cs], scalar=s5D_sb[:, uc:uc + 1], in1=psY[pp][:], op0=ALU.mult, op1=ALU.add), [uch, s5D_sb, psY[pp]], [tmp5[1]])
                    P.add("act", lambda e, yb=yb: e.activation(out=yb[:], in_=tmp5[1][:], func=AF.Gelu_apprx_tanh), reads=[tmp5[1]], writes=[yb])
                    finals.append(P.dma(ygT.t.ap()[uc * 128:(uc + 1) * 128, cs], yb[:], reads=[yb]))
    P.emit(finals)
    return nc, P


def prep_odd(inp, b, half, xb_full=None):
    x = inp["x"] if xb_full is None else None
    W = inp["w_in_o"][0]
    d = {}
    d["xT"] = np.ascontiguousarray((x[b] if xb_full is None else xb_full).T)
    cols = []
    for j in range(8):
        h = half * 8 + j
        for base in (0, 2048, 4096):
            cols.append(base + h * 128)
    for c in range(4):
        cols.append(6144 + half * 512 + c * 128)
    Wg = np.stack([W[:, c:c + 128] for c in cols])
    d["Wg"] = np.ascontiguousarray(Wg.reshape(28, 32, 128, 128).transpose(0, 2, 1, 3)).reshape(28, 128, 4096)
    m = np.zeros((128, 4, 512), np.float32)
    s = np.arange(128)[:, None]
    t = np.arange(512)[None, :]
    for o in range(4):
        m[:, o, :] = (o * 128 + s < t)
    d["masks"] = m
    cst = np.zeros((128, 3, 128), np.float32)
    cst[:, 0, :] = np.eye(128)
    cst[:, 1, :] = 1.0
    jj, ss = np.meshgrid(np.arange(128), np.arange(128), indexing="ij")
    cst[:, 2, :] = (jj > ss)
    d["cst"] = cst
    gs = slice(half * 32, half * 32 + 32)

    def gn_q(a):
        sh = a.shape[2:]
        a = a.reshape((16, 2, 64) + sh)
        a = a.reshape((16, 128) + sh)
        return np.ascontiguousarray(np.moveaxis(a, 0, 1))
    A_re = inp["s5_A_re"][0][gs]
    A_im = inp["s5_A_im"][0][gs]
    ldt = np.broadcast_to(inp["s5_log_dt"][0][gs][:, None], (32, 64))
    d["s5p"] = np.ascontiguousarray(np.stack([gn_q(A_re), gn_q(A_im), gn_q(np.ascontiguousarray(ldt))], axis=1))
    d["s5B"] = np.ascontiguousarray(np.stack([gn_q(inp["s5_B_re"][0][gs]), gn_q(inp["s5_B_im"][0][gs])], axis=1))
    Cre = inp["s5_C_re"][0][gs].transpose(0, 2, 1)
    Cim = inp["s5_C_im"][0][gs].transpose(0, 2, 1)
    d["s5C"] = np.ascontiguousarray(np.stack([gn_q(np.ascontiguousarray(Cre)), gn_q(np.ascontiguousarray(Cim))], axis=1))
    d["s5D"] = np.ascontiguousarray(inp["s5_D"][0][half * 512:half * 512 + 512].reshape(4, 128).T)
    glm = np.zeros((128, 2), np.float32)
    glm[:64, 0] = 1.0
    glm[64:, 1] = 1.0
    d["glm"] = glm
    return d


def _run(nc, maps):
    res = run_bass_kernel_spmd(nc, maps, core_ids=list(range(8)))
    return res.results


def kernel(**inp):
    inp = {k: np.asarray(v) for k, v in inp.items()}
    x = inp["x"]
    B, S, D = x.shape
    nc, _ = build_even()
    maps = [prep_even(inp, i // 2, i % 2) for i in range(8)]
    r = _run(nc, maps)
    del maps
    mixedT = []
    for b in range(4):
        parts = [r[2 * b]["gdn_out"].T, r[2 * b + 1]["gdn_out"].T, r[2 * b]["rgT"], r[2 * b + 1]["rgT"]]
        mixedT.append(np.concatenate(parts, axis=0))
    w = prep_tail_weights(inp["w_out_e"][0], inp["ln_mix_g"][0], inp["ln_mix_b"][0], inp["ln_ffn_g"][0], inp["ln_ffn_b"][0],
                          inp["peer_wq"][0], inp["peer_keys"][0], inp["peer_u"][0], inp["peer_v"][0])
    nc, _ = build_tail(odd=False)
    maps = []
    for i in range(8):
        b, th = i // 2, i % 2
        m = dict(w)
        m["mixedT"] = np.ascontiguousarray(mixedT[b][:, th * 1024:(th + 1) * 1024])
        m["hprevT"] = np.ascontiguousarray(x[b, th * 1024:(th + 1) * 1024].T)
        maps.append(m)
    r = _run(nc, maps)
    del maps, w, mixedT
    h = np.empty((B, S, D), np.float32)
    for i in range(8):
        b, th = i // 2, i % 2
        h[b, th * 1024:(th + 1) * 1024] = r[i]["outT"].T
    nc, _ = build_odd()
    maps = [prep_odd(inp, i // 2, i % 2, xb_full=h[i // 2]) for i in range(8)]
    r = _run(nc, maps)
    del maps
    mixedT = []
    for b in range(4):
        parts = [r[2 * b]["sbT"], r[2 * b + 1]["sbT"], r[2 * b]["ygT"], r[2 * b + 1]["ygT"]]
        mixedT.append(np.concatenate(parts, axis=0))
    w = prep_tail_weights(inp["w_out_o"][0], inp["ln_mix_g"][1], inp["ln_mix_b"][1], inp["ln_ffn_g"][1], inp["ln_ffn_b"][1],
                          inp["peer_wq"][1], inp["peer_keys"][1], inp["peer_u"][1], inp["peer_v"][1],
                          glu_w=inp["s5_glu_w"][0], glu_b=inp["s5_glu_b"][0])
    nc, _ = build_tail(odd=True)
    maps = []
    for i in range(8):
        b, th = i // 2, i % 2
        m = dict(w)
        m["mixedT"] = np.ascontiguousarray(mixedT[b][:, th * 1024:(th + 1) * 1024])
        m["hprevT"] = np.ascontiguousarray(h[b, th * 1024:(th + 1) * 1024].T)
        maps.append(m)
    r = _run(nc, maps)
    out = np.empty((B, S, D), np.float32)
    for i in range(8):
        b, th = i // 2, i % 2
        out[b, th * 1024:(th + 1) * 1024] = r[i]["outT"].T
    return out
```
